# Optimizing a Trainium2 kernel written in Bass

```python
import jax, jax.numpy as jnp
from jax import lax
import numpy as np

D_MODEL = 4096
BATCH = 4
SEQ = 4096
DEPTH = 1

N_MEM = 256
EPS = 1e-6

GLA_HEADS = 4
GLA_DK = 256
GLA_DV = 512
GLA_RANK = 16
GLA_TAU = 16.0
GLA_CHUNK = 16
GLA_QK = GLA_HEADS * GLA_DK
GLA_V = GLA_HEADS * GLA_DV

CONV_WIDTH = 2048
CONV_K = 3

XA_HEADS = 4
XA_DH = 512
XA_W = XA_HEADS * XA_DH

N_BRANCH = 3

PEER_HEADS = 8
PEER_NKEYS = 128
PEER_DQ = 256
PEER_TOPK = 16
PEER_EXPERTS = PEER_NKEYS * PEER_NKEYS
PEER_BLOCK = 128

SPLIT_SIZES = (GLA_QK, GLA_QK, GLA_V, GLA_V, GLA_RANK, CONV_WIDTH, CONV_WIDTH, CONV_WIDTH, XA_W, N_BRANCH * D_MODEL)
IN_WIDTH = sum(SPLIT_SIZES)

kernel_name = "hybrid_gla_shortconv_memxattn_peer"


def rms_norm(x, g):
    xf = x.astype(jnp.float32)
    y = xf * lax.rsqrt(jnp.mean(xf * xf, axis=-1, keepdims=True) + EPS)
    return (y * g.astype(jnp.float32)).astype(x.dtype)


def gla_chunked(q, k, v, log_a):
    B, S, H, dk = q.shape
    dv = v.shape[-1]
    n = S // GLA_CHUNK

    def to_chunks(t):
        return t.reshape(B, n, GLA_CHUNK, H, t.shape[-1]).transpose(1, 0, 3, 2, 4).astype(jnp.float32)

    qc, kc, vc, ac = to_chunks(q), to_chunks(k), to_chunks(v), to_chunks(log_a)
    b = jnp.cumsum(ac, axis=3)
    b_last = b[:, :, :, -1:, :]
    q_t = qc * jnp.exp(b) * (dk ** -0.5)
    k_in = kc * jnp.exp(-b)
    k_out = kc * jnp.exp(b_last - b)
    decay = jnp.exp(b_last[:, :, :, 0, :])

    causal = jnp.tril(jnp.ones((GLA_CHUNK, GLA_CHUNK), dtype=bool))
    att = jnp.einsum('nbhtd,nbhsd->nbhts', q_t, k_in)
    att = jnp.where(causal, att, 0.0)
    o_intra = jnp.einsum('nbhts,nbhsv->nbhtv', att, vc)

    def step(state, inp):
        q_i, k_i, v_i, d_i = inp
        o = jnp.einsum('bhtd,bhdv->bhtv', q_i, state)
        state = d_i[..., None] * state + jnp.einsum('bhsd,bhsv->bhdv', k_i, v_i)
        return state, o

    s0 = jnp.zeros((B, H, dk, dv), jnp.float32)
    _, o_inter = lax.scan(step, s0, (q_t, k_out, vc, decay))
    o = (o_intra + o_inter).transpose(1, 0, 3, 2, 4).reshape(B, S, H, dv)
    return o


def causal_short_conv(u, w):
    S = u.shape[1]
    up = jnp.pad(u, ((0, 0), (CONV_K - 1, 0), (0, 0)))
    y = w[0] * up[:, CONV_K - 1:CONV_K - 1 + S]
    for j in range(1, CONV_K):
        y = y + w[j] * up[:, CONV_K - 1 - j:CONV_K - 1 - j + S]
    return y


def memory_cross_attention(q, mem_kv):
    B, S, _ = q.shape
    M = mem_kv.shape[1]
    qh = q.reshape(B, S, XA_HEADS, XA_DH)
    kh = mem_kv[..., :XA_W].reshape(B, M, XA_HEADS, XA_DH)
    vh = mem_kv[..., XA_W:].reshape(B, M, XA_HEADS, XA_DH)
    s = jnp.einsum('bshd,bmhd->bhsm', qh, kh).astype(jnp.float32) * (XA_DH ** -0.5)
    p = jax.nn.softmax(s, axis=-1).astype(vh.dtype)
    o = jnp.einsum('bhsm,bmhd->bshd', p, vh)
    return o.reshape(B, S, XA_W)


def peer(h, w_q, sub_keys, u_tab, v_tab):
    B, S, D = h.shape
    T = B * S
    hf = h.reshape(T, D)
    q = (hf @ w_q).reshape(T, PEER_HEADS, 2, PEER_DQ // 2)
    s = jnp.einsum('thcd,hckd->thck', q, sub_keys).astype(jnp.float32)
    sv, si = lax.top_k(s, PEER_TOPK)
    cand = sv[:, :, 0, :, None] + sv[:, :, 1, None, :]
    cand_idx = si[:, :, 0, :, None] * PEER_NKEYS + si[:, :, 1, None, :]
    cv, ci = lax.top_k(cand.reshape(T, PEER_HEADS, PEER_TOPK * PEER_TOPK), PEER_TOPK)
    eidx = jnp.take_along_axis(cand_idx.reshape(T, PEER_HEADS, PEER_TOPK * PEER_TOPK), ci, axis=-1)
    g = jax.nn.softmax(cv, axis=-1)

    nb = T // PEER_BLOCK
    xb = hf.reshape(nb, PEER_BLOCK, D)
    ib = eidx.reshape(nb, PEER_BLOCK, PEER_HEADS * PEER_TOPK)
    gb = g.reshape(nb, PEER_BLOCK, PEER_HEADS * PEER_TOPK)

    def block(args):
        x_i, idx_i, g_i = args
        u_sel = u_tab[idx_i]
        a = jnp.einsum('tkd,td->tk', u_sel, x_i).astype(jnp.float32)
        wgt = (g_i * jax.nn.gelu(a, approximate=False)).astype(v_tab.dtype)
        v_sel = v_tab[idx_i]
        return jnp.einsum('tk,tkd->td', wgt, v_sel)

    out = lax.map(block, (xb, ib, gb))
    return out.reshape(B, S, D).astype(h.dtype)


def setup_inputs(seed: int = 0) -> dict:
    key = jax.random.key(seed)
    ks = jax.random.split(key, 24)
    f32 = jnp.float32
    D = D_MODEL

    def nrm(k, shape, scale):
        return jax.random.normal(k, shape, f32) * scale

    def gain(k, shape):
        return 1.0 + 0.02 * jax.random.normal(k, shape, f32)

    return {
        "x": jax.random.normal(ks[0], (BATCH, SEQ, D), f32),
        "mem": jax.random.normal(ks[1], (BATCH, N_MEM, D), f32),
        "norm_mix": gain(ks[2], (DEPTH, D)),
        "w_in": nrm(ks[3], (DEPTH, D, IN_WIDTH), D ** -0.5),
        "w_a_up": nrm(ks[4], (DEPTH, GLA_RANK, GLA_QK), GLA_RANK ** -0.5),
        "b_a": nrm(ks[5], (DEPTH, GLA_QK), 0.1),
        "gla_norm": gain(ks[6], (DEPTH, GLA_DV)),
        "conv_w": nrm(ks[7], (DEPTH, CONV_K, CONV_WIDTH), CONV_K ** -0.5),
        "w_br_gla": nrm(ks[8], (DEPTH, GLA_V, D), GLA_V ** -0.5),
        "w_br_conv": nrm(ks[9], (DEPTH, CONV_WIDTH, D), CONV_WIDTH ** -0.5),
        "w_mem_kv": nrm(ks[10], (DEPTH, D, 2 * XA_W), D ** -0.5),
        "w_br_xa": nrm(ks[11], (DEPTH, XA_W, D), XA_W ** -0.5),
        "b_gate": nrm(ks[12], (DEPTH, N_BRANCH * D), 0.1),
        "w_o": nrm(ks[13], (DEPTH, D, D), D ** -0.5),
        "mem_norm": gain(ks[14], (D,)),
        "norm_ffn": gain(ks[15], (DEPTH, D)),
        "peer_wq": nrm(ks[16], (DEPTH, D, PEER_HEADS * PEER_DQ), D ** -0.5),
        "peer_subkeys": nrm(ks[17], (DEPTH, PEER_HEADS, 2, PEER_NKEYS, PEER_DQ // 2), (PEER_DQ // 2) ** -0.5),
        "peer_u": nrm(ks[18], (DEPTH, PEER_EXPERTS, D), D ** -0.5),
        "peer_v": nrm(ks[19], (DEPTH, PEER_EXPERTS, D), (PEER_HEADS * PEER_TOPK) ** -0.5),
        "final_norm": gain(ks[20], (D,)),
    }


def reference(x, mem, norm_mix, w_in, w_a_up, b_a, gla_norm, conv_w, w_br_gla, w_br_conv,
              w_mem_kv, w_br_xa, b_gate, w_o, mem_norm, norm_ffn, peer_wq, peer_subkeys,
              peer_u, peer_v, final_norm):
    B, S, D = x.shape
    split_at = [int(c) for c in np.cumsum(SPLIT_SIZES)[:-1]]
    mem_n = rms_norm(mem, mem_norm)

    for l in range(DEPTH):
        hn = rms_norm(x, norm_mix[l])
        proj = hn @ w_in[l]
        q_g, k_g, v_g, r_g, a_lr, c_b, c_c, c_h, q_x, gates = jnp.split(proj, split_at, axis=-1)

        log_a = jax.nn.log_sigmoid((a_lr @ w_a_up[l] + b_a[l]).astype(jnp.float32)) / GLA_TAU
        o_g = gla_chunked(q_g.reshape(B, S, GLA_HEADS, GLA_DK),
                          k_g.reshape(B, S, GLA_HEADS, GLA_DK),
                          v_g.reshape(B, S, GLA_HEADS, GLA_DV),
                          log_a.reshape(B, S, GLA_HEADS, GLA_DK))
        o_g = rms_norm(o_g, gla_norm[l]).reshape(B, S, GLA_V).astype(x.dtype)
        y_gla = (o_g * jax.nn.silu(r_g)) @ w_br_gla[l]

        y_conv = (c_b * causal_short_conv(c_c * c_h, conv_w[l])) @ w_br_conv[l]

        mem_kv = mem_n @ w_mem_kv[l]
        y_xa = memory_cross_attention(q_x, mem_kv) @ w_br_xa[l]

        g = jax.nn.sigmoid((gates + b_gate[l]).astype(jnp.float32)).astype(x.dtype).reshape(B, S, N_BRANCH, D)
        merged = g[:, :, 0] * y_gla + g[:, :, 1] * y_conv + g[:, :, 2] * y_xa
        x = x + merged @ w_o[l]

        x = x + peer(rms_norm(x, norm_ffn[l]), peer_wq[l], peer_subkeys[l], peer_u[l], peer_v[l])

    return rms_norm(x, final_norm)
```

```python
import numpy as np
from contextlib import ExitStack
import concourse.bass as bass
import concourse.mybir as mybir
from concourse.bass_utils import run_bass_kernel_spmd

F32 = mybir.dt.float32
BF16 = mybir.dt.bfloat16
AF = mybir.ActivationFunctionType
ALU = mybir.AluOpType
P = 128
EPS = 1e-6
NEG_BIG = 1.0e4


class Cfg:
    def __init__(s, D=4096, N=2048, NPRE=2048, NMEM=256, GH=4, CW=2048, XH=4, PH=8, stages=99):
        s.D, s.N, s.NPRE, s.NMEM = D, N, NPRE, NMEM
        s.GH, s.DK, s.DV, s.RANK = GH, 256, 512, 16
        s.QK, s.GV = GH * 256, GH * 512
        s.CW, s.XH, s.DH, s.XW = CW, XH, 512, XH * 512
        s.PH, s.NK, s.PQ, s.NE = PH, 128, PH * 256, 128 * 128
        s.KC = D // P
        s.TG = min(512, N)
        s.HALO = min(512, NPRE)
        s.NT = NPRE + N
        sizes = [s.QK, s.QK, s.GV, s.GV, 16, CW, CW, CW, s.XW, 3 * D]
        s.off = [0]
        for z in sizes:
            s.off.append(s.off[-1] + z)
        s.INW = s.off[-1]
        s.stages = stages


class Tok:
    def __init__(self, name):
        self.name = name
        self.w = None
        self.r = {}
        self.sem = None
        self.total = 0
        self.dw = {}
        self.dr = {}


class Sched:
    ENG = ['pe', 'act', 'dve', 'pool', 'sp']

    def __init__(self, nc, stack):
        self.nc = nc
        self.stack = stack
        self.eng = {'pe': nc.tensor, 'act': nc.scalar, 'dve': nc.vector, 'pool': nc.gpsimd, 'sp': nc.sync}
        self.sem, self.tick, self.seen, self.seen_d = {}, {}, {}, {}
        self.semtoks = []
        self.free = []
        self.scopes = []
        for e in self.ENG:
            self.sem[e] = stack.enter_context(nc.semaphore('s_' + e))
            self.tick[e] = 0
            self.seen[e] = {}
            self.seen_d[e] = {}

    def tok(self, name):
        t = Tok(name)
        if self.scopes:
            self.scopes[-1].append(t)
        return t

    def toks(self, name, n):
        return [self.tok('%s%d' % (name, i)) for i in range(n)]

    def begin(self):
        self.scopes.append([])

    def end(self):
        self.barrier()
        for t in self.scopes.pop():
            if t.sem is not None:
                self.free.append((t.sem, t.total))
                self.semtoks.remove(t)
                t.sem = None

    def _wait_eng(self, e, oe, t):
        if t <= 0 or self.seen[e].get(oe, 0) >= t:
            return
        self.eng[e].wait_ge(self.sem[oe], t)
        self.seen[e][oe] = t

    def _wait_dma(self, e, st):
        if st.total <= 0 or self.seen_d[e].get(st, 0) >= st.total:
            return
        self.eng[e].wait_ge(st.sem, st.total)
        self.seen_d[e][st] = st.total

    def _need(self, e, reads, writes):
        for t in reads:
            if t.w is not None:
                self._wait_eng(e, t.w[0], t.w[1])
            for st in t.dw:
                self._wait_dma(e, st)
        for t in writes:
            if t.w is not None:
                self._wait_eng(e, t.w[0], t.w[1])
            for oe, tk in t.r.items():
                self._wait_eng(e, oe, tk)
            for st in t.dw:
                self._wait_dma(e, st)
            for st in t.dr:
                self._wait_dma(e, st)

    def op(self, e, fn, reads=(), writes=()):
        self._need(e, reads, writes)
        ins = fn(self.eng[e])
        self.tick[e] += 1
        ins.then_inc(self.sem[e], 1)
        tk = self.tick[e]
        for t in reads:
            t.r[e] = tk
        for t in writes:
            t.w = (e, tk)
            t.r = {}
            t.dw = {}
            t.dr = {}
        return ins

    def dma(self, q, out, in_, reads=(), writes=(), semtok=None, **kw):
        self._need(q, reads, writes)
        ins = self.eng[q].dma_start(out=out, in_=in_, **kw)
        st = semtok
        if st.sem is None:
            if self.free:
                st.sem, st.total = self.free.pop()
            else:
                self.nsem = getattr(self, 'nsem', 0) + 1
                st.sem = self.stack.enter_context(self.nc.semaphore('d%d' % self.nsem))
                st.total = 0
            self.semtoks.append(st)
        st.total += 16
        ins.then_inc(st.sem, 16)
        for t in reads:
            t.dr[st] = True
        for t in writes:
            t.dw[st] = True
            t.w = None
            t.r = {}
        return ins

    def barrier(self):
        for e in self.ENG:
            for oe in self.ENG:
                if oe != e:
                    self._wait_eng(e, oe, self.tick[oe])
            for st in self.semtoks:
                self._wait_dma(e, st)

    def finish(self):
        for st in self.semtoks:
            self._wait_dma('sp', st)
        for oe in self.ENG:
            if oe != 'sp':
                self._wait_eng('sp', oe, self.tick[oe])


def build(cfg, dbg=()):
    nc = bass.Bass("TRN2", target_bir_lowering=False)
    D, N, NPRE, NT, NMEM, KC, TG = cfg.D, cfg.N, cfg.NPRE, cfg.NT, cfg.NMEM, cfg.KC, cfg.TG
    QK, GV, CW, XW, PQ, NE, GH, PH, XH = cfg.QK, cfg.GV, cfg.CW, cfg.XW, cfg.PQ, cfg.NE, cfg.GH, cfg.PH, cfg.XH
    HALO = cfg.HALO
    NG = N // TG

    uid = [0]

    def un(name):
        uid[0] += 1
        return "%s_%d" % (name, uid[0])

    def din(name, shape, dt=F32):
        return nc.dram_tensor(name, list(shape), dt, kind="ExternalInput").ap()

    def dscr(name, shape, dt=F32):
        kind = "ExternalOutput" if name in dbg else "Internal"
        return nc.dram_tensor(name, list(shape), dt, kind=kind).ap()

    x_main = din("x_main", [N, D])
    x_pre = din("x_pre", [NPRE, D])
    mem = din("mem", [NMEM, D])
    norm_mix = din("norm_mix", [1, D])
    w_in = din("w_in", [D, cfg.INW])
    w_aug = din("w_aug", [17, QK])
    gla_norm = din("gla_norm", [4, P])
    conv_w = din("conv_w", [3 * CW // P, P])
    w_br_gla = din("w_br_gla", [GV, D])
    w_br_conv = din("w_br_conv", [CW, D])
    w_mem_kv = din("w_mem_kv", [D, 2 * XW])
    w_br_xa = din("w_br_xa", [XW, D])
    b_gate = din("b_gate", [3 * D // P, P])
    w_o = din("w_o", [D, D])
    mem_norm = din("mem_norm", [1, D])
    norm_ffn = din("norm_ffn", [1, D])
    peer_wq = din("peer_wq", [D, PQ])
    peer_sk = din("peer_sk", [2 * PH * P, P])
    peer_u = din("peer_u", [NE, D])
    peer_v = din("peer_v", [NE, D])
    final_norm = din("final_norm", [1, D])
    cst = din("cst", [P, 5 * P])
    out = nc.dram_tensor("out", [N, D], F32, kind="ExternalOutput").ap()

    s_q = dscr("s_q", [QK, N])
    s_k = dscr("s_k", [QK, NT])
    s_v = dscr("s_v", [GV, NT], BF16)
    s_r = dscr("s_r", [GV, N])
    s_alr = dscr("s_alr", [16, NT])
    s_cb = dscr("s_cb", [CW, N])
    s_cc = dscr("s_cc", [CW, HALO + N])
    s_ch = dscr("s_ch", [CW, HALO + N])
    s_qx = dscr("s_qx", [XW, N], BF16)
    s_g = dscr("s_g", [3 * D, N])
    s_mkv = dscr("s_mkv", [2 * XW, NMEM], BF16)
    s_ogr = dscr("s_ogr", [GV, N], BF16)
    s_zc = dscr("s_zc", [CW, N], BF16)
    s_ox = dscr("s_ox", [XW, N], BF16)
    s_x2 = dscr("s_x2", [N, D])
    s_qp = dscr("s_qp", [PQ, N], BF16)
    s_wt = dscr("s_wt", [NE, N], BF16)
    s_wg = dscr("s_wg", [NE, N], BF16)
    s_x3 = dscr("s_x3", [N, D])

    with ExitStack() as st0:
        S = Sched(nc, st0)
        sb = lambda name, shape, dt=F32: st0.enter_context(nc.sbuf_tensor(name, list(shape), dt))
        PS = [st0.enter_context(nc.psum_tensor("ps%d" % i, [P, 512], F32)) for i in range(8)]
        tPS = S.toks("ps", 8)
        PSB = [p[:].bitcast(BF16) for p in PS]
        cf = sb("cf", [P, 5 * P])
        cb16 = sb("cb16", [P, 5 * P], BF16)
        epsT = sb("epsT", [P, 1])
        oneT = sb("oneT", [P, 1])
        t_c = S.tok("consts")
        S.dma('sp', cf[:], cst[:, :], writes=[t_c], semtok=t_c)
        S.op('dve', lambda e: e.tensor_copy(out=cb16[:], in_=cf[:]), reads=[t_c], writes=[t_c])
        S.op('dve', lambda e: e.memset(epsT[:], EPS), writes=[t_c])
        S.op('dve', lambda e: e.memset(oneT[:], 1.0), writes=[t_c])
        identf, triU, triL, mask01f, onesf = [cf[:, i * P:(i + 1) * P] for i in range(5)]
        identb, _, _, mask01b, onesb = [cb16[:, i * P:(i + 1) * P] for i in range(5)]

        def rows_view(ap2d, r0, nchunk, c0, ncol):
            return ap2d[r0:r0 + nchunk * P, c0:c0 + ncol].rearrange("(c p) n -> p c n", p=P)

        def load_cols(dst, src, n, name):
            with ExitStack() as st:
                S.begin()
                tmp = st.enter_context(nc.sbuf_tensor(un("lc_" + name), [P, P], F32))
                tt = S.tok("lc_" + name)
                S.dma('sp', tmp[:n, :], src[0:n, :], writes=[tt], semtok=tt)
                S.op('pe', lambda e: e.transpose(out=PS[0][:, :n], in_=tmp[:n, :], identity=identf[:n, :n]),
                     reads=[tt, t_c], writes=[tPS[0]])
                S.op('dve', lambda e: e.tensor_copy(out=dst, in_=PS[0][:, :n]), reads=[tPS[0]], writes=[t_c])
                S.end()

        def norm_T(src, gain_ap, actT, t_act, ntok, col0=0):
            with ExitStack() as st:
                S.begin()
                lsb = lambda name, shape, dt=F32: st.enter_context(nc.sbuf_tensor(un(name), list(shape), dt))
                gb = lsb("nt_gb", [P, D])
                xs = [lsb("nt_xs%d" % i, [P, D]) for i in range(2)]
                xb = lsb("nt_xb", [P, D], BF16)
                junk = lsb("nt_junk", [P, D], BF16)
                ss = lsb("nt_ss", [P, 4])
                t_gb, t_xb, t_junk, t_ss = S.tok("gb"), S.tok("xb"), S.tok("junk"), S.tok("ss")
                t_xs = S.toks("xs", 2)
                S.dma('sp', gb[:], gain_ap[0:1, :].to_broadcast([P, D]), writes=[t_gb], semtok=t_gb)
                nt = ntok // P
                S.dma('sp', xs[0][:], src[0:P, :], writes=[t_xs[0]], semtok=t_xs[0])
                for i in range(nt):
                    sl = i % 2
                    if i + 1 < nt:
                        S.dma('sp', xs[1 - sl][:], src[(i + 1) * P:(i + 2) * P, :], writes=[t_xs[1 - sl]],
                              semtok=t_xs[1 - sl])
                    S.op('act', lambda e: e.activation(out=junk[:], in_=xs[sl][:], func=AF.Square,
                                                       accum_out=ss[:, 0:1]),
                         reads=[t_xs[sl]], writes=[t_junk, t_ss])
                    S.op('act', lambda e: e.activation(out=ss[:, 1:2], in_=ss[:, 0:1], func=AF.Sqrt,
                                                       scale=1.0 / D, bias=epsT[:, 0:1]),
                         reads=[t_ss, t_c], writes=[t_ss])
                    S.op('dve', lambda e: e.reciprocal(out=ss[:, 2:3], in_=ss[:, 1:2]), reads=[t_ss], writes=[t_ss])
                    S.op('dve', lambda e: e.scalar_tensor_tensor(out=xb[:], in0=xs[sl][:], scalar=ss[:, 2:3],
                                                                 in1=gb[:], op0=ALU.mult, op1=ALU.mult),
                         reads=[t_xs[sl], t_ss, t_gb], writes=[t_xb])
                    for c8 in range(0, KC, 8):
                        nb = min(8, KC - c8)
                        bk = (c8 // 8) % 2

                        def tr(e, c8=c8, nb=nb, bk=bk):
                            for k in range(nb):
                                ins = e.transpose(out=PSB[bk][:, k * P:(k + 1) * P],
                                                  in_=xb[:, (c8 + k) * P:(c8 + k + 1) * P], identity=identb)
                            return ins
                        S.op('pe', tr, reads=[t_xb, t_c], writes=[tPS[bk]])
                        eng = 'act' if (c8 // 8) % 2 == 0 else 'dve'
                        dst = actT[:, c8:c8 + nb, col0 + i * P:col0 + (i + 1) * P]
                        srcp = PSB[bk][:, :nb * P].rearrange("p (c n) -> p c n", n=P)
                        if eng == 'act':
                            S.op('act', lambda e, dst=dst, srcp=srcp: e.copy(out=dst, in_=srcp),
                                 reads=[tPS[bk]], writes=[t_act])
                        else:
                            S.op('dve', lambda e, dst=dst, srcp=srcp: e.tensor_copy(out=dst, in_=srcp),
                                 reads=[tPS[bk]], writes=[t_act])
                S.end()

        def proj(w_ap, Kc, actT, t_act, ntok, tcol0, blocks, name):
            with ExitStack() as st:
                S.begin()
                lsb = lambda nm, shape, dt=F32: st.enter_context(nc.sbuf_tensor(un(nm), list(shape), dt))
                KH = min(Kc, 16)
                nh = Kc // KH
                tg = min(512, ntok)
                ng = ntok // tg
                assert ng <= 4
                NW = 3
                wst = [lsb("pj_wst%d" % i, [P, KH, P]) for i in range(NW)]
                wbf = [lsb("pj_wbf%d" % i, [P, KH, P], BF16) for i in range(NW)]
                t_wst, t_wbf = S.toks("wst", NW), S.toks("wbf", NW)
                NO = 4
                ob32 = [lsb("pj_o%d" % i, [P, 512]) for i in range(NO)]
                t_ob = S.toks("pjo", NO)
                tiles = [(b, h) for b in range(len(blocks)) for h in range(nh)]

                def load(ti):
                    b, h = tiles[ti]
                    c0, ncol = blocks[b][0], blocks[b][1]
                    sl = ti % NW
                    S.dma('sp', wst[sl][:, :, :ncol], rows_view(w_ap, h * KH * P, KH, c0, ncol),
                          writes=[t_wst[sl]], semtok=t_wst[sl])
                for ti in range(min(NW - 1, len(tiles))):
                    load(ti)
                oc = 0
                for ti, (b, h) in enumerate(tiles):
                    if ti + NW - 1 < len(tiles):
                        load(ti + NW - 1)
                    c0, ncol, epi, dst, ddt, bias = blocks[b]
                    sl = ti % NW
                    ce = 'dve' if ti % 2 == 0 else 'pool'
                    S.op(ce, lambda e: e.tensor_copy(out=wbf[sl][:, :, :ncol], in_=wst[sl][:, :, :ncol]),
                         reads=[t_wst[sl]], writes=[t_wbf[sl]])
                    pb = (b % 2) * 4

                    def mm(e, h=h, sl=sl, ncol=ncol, pb=pb):
                        for g in range(ng):
                            for c in range(KH):
                                ins = e.matmul(PS[pb + g][:ncol, :tg], lhsT=wbf[sl][:, c, :ncol],
                                               rhs=actT[:, h * KH + c, tcol0 + g * tg:tcol0 + (g + 1) * tg],
                                               start=(h == 0 and c == 0), stop=(h == nh - 1 and c == KH - 1))
                        return ins
                    S.op('pe', mm, reads=[t_wbf[sl], t_act], writes=[tPS[pb + g] for g in range(ng)])
                    if h == nh - 1:
                        for g in range(ng):
                            o = oc % NO
                            oc += 1
                            if ddt == BF16:
                                ot = ob32[o][:].bitcast(BF16)[:ncol, :tg]
                            else:
                                ot = ob32[o][:ncol, :tg]
                            src = PS[pb + g][:ncol, :tg]
                            if epi == 'copy':
                                S.op('act', lambda e, ot=ot, src=src: e.copy(out=ot, in_=src),
                                     reads=[tPS[pb + g]], writes=[t_ob[o]])
                            elif epi == 'silu':
                                S.op('act', lambda e, ot=ot, src=src: e.activation(out=ot, in_=src, func=AF.Silu),
                                     reads=[tPS[pb + g]], writes=[t_ob[o]])
                            elif epi == 'sigb':
                                S.op('act', lambda e, ot=ot, src=src, bias=bias: e.activation(
                                    out=ot, in_=src, func=AF.Sigmoid, bias=bias),
                                     reads=[tPS[pb + g], t_c], writes=[t_ob[o]])
                            S.dma('pool', dst[:, g * tg:(g + 1) * tg], ot, reads=[t_ob[o]], semtok=t_ob[o])
                S.end()

        bg_col = sb("bg_col", [P, 3 * D // P])
        cw_col = sb("cw_col", [P, 3 * CW // P])
        gn_col = sb("gn_col", [P, 4])
        load_cols(bg_col[:, :], b_gate, 3 * D // P, "bg")
        load_cols(cw_col[:, :], conv_w, 3 * CW // P, "cw")
        load_cols(gn_col[:, :], gla_norm, 4, "gn")

        o = cfg.off

        def blocks_for(lo, hi, epi, dst, ddt=F32, bias_fn=None):
            res = []
            c = lo
            while c < hi:
                n = min(P, hi - c)
                r0 = c - lo
                res.append((c, n, epi, dst(r0, n), ddt, bias_fn(r0 // P) if bias_fn else None))
                c += n
            return res

        with ExitStack() as st1:
            actT = st1.enter_context(nc.sbuf_tensor("actT", [P, KC, max(N, NPRE, NMEM)], BF16))
            t_act = S.tok("actT")
            norm_T(x_pre, norm_mix, actT, t_act, NPRE)
            bl = []
            bl += blocks_for(o[1], o[2], 'copy', lambda r, n: s_k[r:r + n, 0:NPRE])
            bl += blocks_for(o[2], o[3], 'copy', lambda r, n: s_v[r:r + n, 0:NPRE], BF16)
            bl += blocks_for(o[4], o[5], 'copy', lambda r, n: s_alr[r:r + n, 0:NPRE])
            proj(w_in, KC, actT, t_act, NPRE, 0, bl, "pre")
            bl = []
            bl += blocks_for(o[6], o[7], 'copy', lambda r, n: s_cc[r:r + n, 0:HALO])
            bl += blocks_for(o[7], o[8], 'copy', lambda r, n: s_ch[r:r + n, 0:HALO])
            proj(w_in, KC, actT, t_act, HALO, NPRE - HALO, bl, "pre2")
            if cfg.stages >= 2:
                norm_T(mem, mem_norm, actT, t_act, NMEM)
                bl = blocks_for(0, 2 * XW, 'copy', lambda r, n: s_mkv[r:r + n, 0:NMEM], BF16)
                proj(w_mem_kv, KC, actT, t_act, NMEM, 0, bl, "mkv")
            norm_T(x_main, norm_mix, actT, t_act, N)
            bl = []
            bl += blocks_for(o[0], o[1], 'copy', lambda r, n: s_q[r:r + n, 0:N])
            bl += blocks_for(o[1], o[2], 'copy', lambda r, n: s_k[r:r + n, NPRE:NT])
            bl += blocks_for(o[2], o[3], 'copy', lambda r, n: s_v[r:r + n, NPRE:NT], BF16)
            bl += blocks_for(o[3], o[4], 'silu', lambda r, n: s_r[r:r + n, 0:N])
            bl += blocks_for(o[4], o[5], 'copy', lambda r, n: s_alr[r:r + n, NPRE:NT])
            bl += blocks_for(o[5], o[6], 'copy', lambda r, n: s_cb[r:r + n, 0:N])
            bl += blocks_for(o[6], o[7], 'copy', lambda r, n: s_cc[r:r + n, HALO:HALO + N])
            bl += blocks_for(o[7], o[8], 'copy', lambda r, n: s_ch[r:r + n, HALO:HALO + N])
            bl += blocks_for(o[8], o[9], 'copy', lambda r, n: s_qx[r:r + n, 0:N], BF16)
            bl += blocks_for(o[9], o[10], 'sigb', lambda r, n: s_g[r:r + n, 0:N], F32,
                             lambda j: bg_col[:, j:j + 1])
            proj(w_in, KC, actT, t_act, N, 0, bl, "main")


        if cfg.stages >= 2:
            with ExitStack() as st:
                S.begin()
                lsb = lambda nm, shape, dt=F32: st.enter_context(nc.sbuf_tensor(un(nm), list(shape), dt))
                CWC = CW // P
                cc = [lsb("cv_cc", [P, N + 2]) for _ in range(2)]
                ch = [lsb("cv_ch", [P, N + 2]) for _ in range(2)]
                cbt = [lsb("cv_cb", [P, N]) for _ in range(2)]
                u = lsb("cv_u", [P, N + 2])
                y = lsb("cv_y", [P, N])
                z = [lsb("cv_z", [P, N], BF16) for _ in range(2)]
                t_ld = S.toks("cvld", 2)
                t_u, t_y = S.tok("cvu"), S.tok("cvy")
                t_z = S.toks("cvz", 2)

                def ld(fc):
                    sl = fc % 2
                    r = slice(fc * P, (fc + 1) * P)
                    S.dma('sp', cc[sl][:], s_cc[r, HALO - 2:HALO + N], writes=[t_ld[sl]], semtok=t_ld[sl])
                    S.dma('sp', ch[sl][:], s_ch[r, HALO - 2:HALO + N], writes=[t_ld[sl]], semtok=t_ld[sl])
                    S.dma('sp', cbt[sl][:], s_cb[r, 0:N], writes=[t_ld[sl]], semtok=t_ld[sl])
                ld(0)
                for fc in range(CWC):
                    sl = fc % 2
                    if fc + 1 < CWC:
                        ld(fc + 1)
                    S.op('dve', lambda e: e.tensor_tensor(out=u[:], in0=cc[sl][:], in1=ch[sl][:], op=ALU.mult),
                         reads=[t_ld[sl]], writes=[t_u])
                    S.op('dve', lambda e: e.tensor_scalar(out=y[:], in0=u[:, 2:N + 2],
                                                          scalar1=cw_col[:, 0 * CWC + fc:0 * CWC + fc + 1],
                                                          scalar2=None, op0=ALU.mult),
                         reads=[t_u, t_c], writes=[t_y])
                    S.op('dve', lambda e: e.scalar_tensor_tensor(out=y[:], in0=u[:, 1:N + 1],
                                                                 scalar=cw_col[:, 1 * CWC + fc:1 * CWC + fc + 1],
                                                                 in1=y[:], op0=ALU.mult, op1=ALU.add),
                         reads=[t_u, t_c], writes=[t_y])
                    S.op('dve', lambda e: e.scalar_tensor_tensor(out=y[:], in0=u[:, 0:N],
                                                                 scalar=cw_col[:, 2 * CWC + fc:2 * CWC + fc + 1],
                                                                 in1=y[:], op0=ALU.mult, op1=ALU.add),
                         reads=[t_u, t_c], writes=[t_y])
                    S.op('pool', lambda e: e.tensor_tensor(out=z[sl][:], in0=cbt[sl][:], in1=y[:], op=ALU.mult),
                         reads=[t_ld[sl], t_y], writes=[t_z[sl]])
                    S.dma('pool', s_zc[fc * P:(fc + 1) * P, 0:N], z[sl][:], reads=[t_z[sl]], semtok=t_z[sl])
                S.end()

        if cfg.stages >= 2:
            with ExitStack() as st:
                S.begin()
                lsb = lambda nm, shape, dt=F32: st.enter_context(nc.sbuf_tensor(un(nm), list(shape), dt))
                MC = NMEM // P
                XC = XW // P
                assert MC <= 2
                KT = lsb("xa_kt", [P, XC, NMEM], BF16)
                VT = lsb("xa_vt", [P, XC, NMEM], BF16)
                Vm = lsb("xa_vm", [P, MC, XW], BF16)
                qx = [lsb("xa_qx", [P, 4, TG], BF16) for _ in range(2)]
                pT = lsb("xa_pT", [P, MC, TG], BF16)
                rz = lsb("xa_rz", [P, TG])
                ox = [lsb("xa_ox", [P, 4, TG], BF16) for _ in range(2)]
                t_kv, t_vm, t_pT, t_rz = S.tok("xkv"), S.tok("xvm"), S.tok("xpT"), S.tok("xrz")
                t_qx, t_ox = S.toks("xqx", 2), S.toks("xox", 2)
                S.dma('sp', KT[:], rows_view(s_mkv, 0, XC, 0, NMEM), writes=[t_kv], semtok=t_kv)
                S.dma('sp', VT[:], rows_view(s_mkv, XW, XC, 0, NMEM), writes=[t_kv], semtok=t_kv)
                k = 0
                for mc in range(MC):
                    for f0 in range(0, XC, 8):
                        nb = min(8, XC - f0)
                        bk = k % 2
                        k += 1

                        def tr(e, mc=mc, f0=f0, nb=nb, bk=bk):
                            for i in range(nb):
                                ins = e.transpose(out=PSB[bk][:, i * P:(i + 1) * P],
                                                  in_=VT[:, f0 + i, mc * P:(mc + 1) * P], identity=identb)
                            return ins
                        S.op('pe', tr, reads=[t_kv, t_c], writes=[tPS[bk]])
                        S.op('dve', lambda e: e.tensor_copy(out=Vm[:, mc, f0 * P:(f0 + nb) * P],
                                                            in_=PSB[bk][:, :nb * P]),
                             reads=[tPS[bk]], writes=[t_vm])
                jobs = [(h, g) for h in range(XH) for g in range(NG)]

                def ldq(ji):
                    h, g = jobs[ji]
                    S.dma('sp', qx[ji % 2][:], rows_view(s_qx, h * 512, 4, g * TG, TG),
                          writes=[t_qx[ji % 2]], semtok=t_qx[ji % 2])
                ldq(0)
                for ji, (h, g) in enumerate(jobs):
                    sl = ji % 2
                    if ji + 1 < len(jobs):
                        ldq(ji + 1)

                    def sc(e):
                        for mc in range(MC):
                            for dc in range(4):
                                ins = e.matmul(PS[mc][:, :TG], lhsT=KT[:, h * 4 + dc, mc * P:(mc + 1) * P],
                                               rhs=qx[sl][:, dc, :], start=(dc == 0), stop=(dc == 3))
                        return ins
                    S.op('pe', sc, reads=[t_kv, t_qx[sl]], writes=[tPS[m] for m in range(MC)])

                    def ex(e):
                        for mc in range(MC):
                            ins = e.activation(out=pT[:, mc, :], in_=PS[mc][:, :TG], func=AF.Exp,
                                               scale=float(cfg.DH) ** -0.5)
                        return ins
                    S.op('act', ex, reads=[tPS[m] for m in range(MC)], writes=[t_pT])

                    def zz(e):
                        for mc in range(MC):
                            ins = e.matmul(PS[2][:, :TG], lhsT=onesb, rhs=pT[:, mc, :],
                                           start=(mc == 0), stop=(mc == MC - 1))
                        return ins
                    S.op('pe', zz, reads=[t_pT, t_c], writes=[tPS[2]])
                    S.op('dve', lambda e: e.reciprocal(out=rz[:], in_=PS[2][:, :TG]), reads=[tPS[2]], writes=[t_rz])

                    def ov(e):
                        for dc in range(4):
                            for mc in range(MC):
                                ins = e.matmul(PS[3 + dc][:, :TG],
                                               lhsT=Vm[:, mc, h * 512 + dc * P:h * 512 + (dc + 1) * P],
                                               rhs=pT[:, mc, :], start=(mc == 0), stop=(mc == MC - 1))
                        return ins
                    S.op('pe', ov, reads=[t_pT, t_vm], writes=[tPS[3 + d] for d in range(4)])

                    def nrm(e):
                        for dc in range(4):
                            ins = e.tensor_tensor(out=ox[sl][:, dc, :], in0=PS[3 + dc][:, :TG], in1=rz[:],
                                                  op=ALU.mult)
                        return ins
                    S.op('dve', nrm, reads=[tPS[3 + d] for d in range(4)] + [t_rz], writes=[t_ox[sl]])
                    S.dma('pool', rows_view(s_ox, h * 512, 4, g * TG, TG), ox[sl][:], reads=[t_ox[sl]],
                          semtok=t_ox[sl])
                S.end()


        if cfg.stages >= 3:
            with ExitStack() as st:
                S.begin()
                lsb = lambda nm, shape, dt=F32: st.enter_context(nc.sbuf_tensor(un(nm), list(shape), dt))
                QC, VC = QK // P, GV // P
                NQB = (QK + 511) // 512
                NVB = (GV + 1023) // 1024
                assert NQB <= 2 and NVB <= 2 and GH <= 4
                qw = min(512, QK)
                waug = lsb("g_waug", [17, QK])
                alr = [lsb("g_alr", [17, P]) for _ in range(2)]
                kF = [lsb("g_kF", [P, QC, P]) for _ in range(2)]
                vF = [lsb("g_vF", [P, VC, P], BF16) for _ in range(2)]
                qF = [lsb("g_qF", [P, QC, P]) for _ in range(2)]
                rF = [lsb("g_rF", [P, VC, P]) for _ in range(2)]
                en = lsb("g_en", [P, QK])
                ll = lsb("g_ll", [P, QK])
                ecs = lsb("g_ecs", [P, QK])
                ebs = lsb("g_ebs", [P, QC, P])
                enbs = lsb("g_enbs", [P, QC, P])
                koutb = lsb("g_kout", [P, QK], BF16)
                qtb = lsb("g_qt", [P, QC, P], BF16)
                kinb = lsb("g_kin", [P, QC, P], BF16)
                vtok = lsb("g_vtok", [P, GV], BF16)
                attb = lsb("g_att", [P, GH, P], BF16)
                Sst = lsb("g_S", [P, QC, 512])
                Sbf = lsb("g_Sbf", [P, QC, 512], BF16)
                junk = lsb("g_junk", [P, 512], BF16)
                ssg = lsb("g_ss", [P, 3 * GH])
                onb = lsb("g_on", [P, GV], BF16)
                ogr = [lsb("g_ogr", [P, VC, P], BF16) for _ in range(2)]
                t_waug = S.tok("gwaug")
                t_ld = S.toks("gld", 2)
                t_en, t_ll, t_ecs, t_ebs, t_enbs = S.tok("gen"), S.tok("gll"), S.tok("gecs"), S.tok("gebs"), S.tok("genbs")
                t_kout, t_qt, t_kin, t_vtok, t_att = S.tok("gkout"), S.tok("gqt"), S.tok("gkin"), S.tok("gvtok"), S.tok("gatt")
                t_S, t_Sbf = S.toks("gS", QC), S.toks("gSbf", QC)
                t_junk, t_ss, t_on = S.tok("gjunk"), S.tok("gss"), S.tok("gon")
                t_ogr = S.toks("gogr", 2)
                S.dma('sp', waug[:], w_aug[:, :], writes=[t_waug], semtok=t_waug)
                for i in range(2):
                    S.op('dve', lambda e: e.memset(alr[i][:], 1.0), writes=[t_ld[i]])
                S.op('dve', lambda e: e.memset(Sst[:], 0.0), writes=t_S)
                S.op('pool', lambda e: e.memset(Sbf[:], 0.0), writes=t_Sbf)
                nchunks = NT // P

                def ld(ci):
                    sl = ci % 2
                    t0 = ci * P
                    S.dma('sp', alr[sl][0:16, :], s_alr[0:16, t0:t0 + P], writes=[t_ld[sl]], semtok=t_ld[sl])
                    S.dma('sp', kF[sl][:], rows_view(s_k, 0, QC, t0, P), writes=[t_ld[sl]], semtok=t_ld[sl])
                    S.dma('sp', vF[sl][:], rows_view(s_v, 0, VC, t0, P), writes=[t_ld[sl]], semtok=t_ld[sl])
                    if t0 >= NPRE:
                        tm = t0 - NPRE
                        S.dma('sp', qF[sl][:], rows_view(s_q, 0, QC, tm, P), writes=[t_ld[sl]], semtok=t_ld[sl])
                        S.dma('sp', rF[sl][:], rows_view(s_r, 0, VC, tm, P), writes=[t_ld[sl]], semtok=t_ld[sl])
                ld(0)
                for ci in range(nchunks):
                    sl = ci % 2
                    t0 = ci * P
                    main = t0 >= NPRE
                    tm = t0 - NPRE
                    if ci + 1 < nchunks:
                        ld(ci + 1)
                    def f_z(e):
                        for j in range(NQB):
                            ins = e.matmul(PS[j][:, :qw], lhsT=alr[sl][0:17, :], rhs=waug[0:17, j * qw:(j + 1) * qw],
                                           start=True, stop=True)
                        return ins
                    S.op('pe', f_z, reads=[t_ld[sl], t_waug], writes=tPS[0:NQB])
                    def f_en(e):
                        for j in range(NQB):
                            ins = e.activation(out=en[:, j * qw:(j + 1) * qw], in_=PS[j][:, :qw], func=AF.Exp, scale=-1.0)
                        return ins
                    S.op('act', f_en, reads=tPS[0:NQB], writes=[t_en])
                    S.op('act', lambda e: e.activation(out=ll[:], in_=en[:], func=AF.Ln, bias=oneT[:, 0:1]),
                         reads=[t_en, t_c], writes=[t_ll])
                    def f_cum(e):
                        for j in range(NQB):
                            ins = e.matmul(PS[2 + j][:, :qw], lhsT=triL, rhs=ll[:, j * qw:(j + 1) * qw],
                                           start=True, stop=True)
                        for dc in range(QC):
                            ins = e.matmul(PS[4 + dc // 4][:, (dc % 4) * P:(dc % 4 + 1) * P],
                                           lhsT=ll[:, dc * P:(dc + 1) * P], rhs=triU, start=True, stop=True)
                        for dc in range(QC):
                            ins = e.transpose(out=PS[6 + dc // 4][:, (dc % 4) * P:(dc % 4 + 1) * P],
                                              in_=kF[sl][:, dc, :], identity=identf)
                        return ins
                    S.op('pe', f_cum, reads=[t_ll, t_c, t_ld[sl]],
                         writes=tPS[2:2 + NQB] + tPS[4:4 + NQB] + tPS[6:6 + NQB])
                    def f_ec(e):
                        for j in range(NQB):
                            ins = e.activation(out=ecs[:, j * qw:(j + 1) * qw], in_=PS[2 + j][:, :qw], func=AF.Exp)
                        return ins
                    S.op('act', f_ec, reads=tPS[2:2 + NQB], writes=[t_ecs])
                    def f_kout(e):
                        for j in range(NQB):
                            ins = e.tensor_tensor(out=koutb[:, j * qw:(j + 1) * qw], in0=PS[6 + j][:, :qw],
                                                  in1=ecs[:, j * qw:(j + 1) * qw], op=ALU.mult)
                        return ins
                    S.op('dve', f_kout, reads=tPS[6:6 + NQB] + [t_ecs], writes=[t_kout])
                    def f_eb(e, dst=ebs, sc=1.0):
                        for j in range(NQB):
                            n4 = min(4, QC - 4 * j)
                            ins = e.activation(out=dst[:, 4 * j:4 * j + n4, :],
                                               in_=PS[4 + j][:, :n4 * P].rearrange("p (c n) -> p c n", n=P),
                                               func=AF.Exp, scale=sc)
                        return ins
                    S.op('act', f_eb, reads=tPS[4:4 + NQB], writes=[t_ebs])
                    if main:
                        S.op('act', lambda e: f_eb(e, enbs, -1.0), reads=tPS[4:4 + NQB], writes=[t_enbs])
                        S.op('dve', lambda e: e.scalar_tensor_tensor(out=qtb[:], in0=qF[sl][:], scalar=float(cfg.DK) ** -0.5,
                                                                     in1=ebs[:], op0=ALU.mult, op1=ALU.mult),
                             reads=[t_ld[sl], t_ebs], writes=[t_qt])
                        S.op('dve', lambda e: e.tensor_tensor(out=kinb[:], in0=kF[sl][:], in1=enbs[:], op=ALU.mult),
                             reads=[t_ld[sl], t_enbs], writes=[t_kin])
                    def f_vt(e):
                        for vc in range(VC):
                            ins = e.transpose(out=PSB[vc // 8][:, (vc % 8) * P:(vc % 8 + 1) * P],
                                              in_=vF[sl][:, vc, :], identity=identb)
                        return ins
                    S.op('pe', f_vt, reads=[t_ld[sl], t_c], writes=tPS[0:NVB])
                    def f_vc(e):
                        for b in range(NVB):
                            w = min(1024, GV - b * 1024)
                            ins = e.copy(out=vtok[:, b * 1024:b * 1024 + w], in_=PSB[b][:, :w])
                        return ins
                    S.op('act', f_vc, reads=tPS[0:NVB], writes=[t_vtok])
                    if main:
                        def f_att(e):
                            for h in range(GH):
                                for dd in range(2):
                                    ins = e.matmul(PS[2][:, h * P:(h + 1) * P], lhsT=kinb[:, 2 * h + dd, :],
                                                   rhs=qtb[:, 2 * h + dd, :], start=(dd == 0), stop=(dd == 1))
                            return ins
                        S.op('pe', f_att, reads=[t_kin, t_qt], writes=[tPS[2]])
                        def f_mask(e):
                            for h in range(GH):
                                ins = e.tensor_tensor(out=attb[:, h, :], in0=PS[2][:, h * P:(h + 1) * P], in1=mask01f,
                                                      op=ALU.mult)
                            return ins
                        S.op('dve', f_mask, reads=[tPS[2], t_c], writes=[t_att])
                        def f_o(e):
                            for h in range(GH):
                                e.matmul(PS[4 + h][:, :512], lhsT=attb[:, h, :], rhs=vtok[:, h * 512:(h + 1) * 512],
                                         start=True, stop=False)
                                for dd in range(2):
                                    ins = e.matmul(PS[4 + h][:, :512], lhsT=qtb[:, 2 * h + dd, :],
                                                   rhs=Sbf[:, 2 * h + dd, :], start=False, stop=(dd == 1))
                            return ins
                        S.op('pe', f_o, reads=[t_att, t_vtok, t_qt] + t_Sbf, writes=tPS[4:4 + GH])
                        def f_sq(e):
                            for h in range(GH):
                                ins = e.activation(out=junk[:], in_=PS[4 + h][:, :512], func=AF.Square,
                                                   accum_out=ssg[:, h:h + 1])
                            return ins
                        S.op('act', f_sq, reads=tPS[4:4 + GH], writes=[t_junk, t_ss])
                        S.op('act', lambda e: e.activation(out=ssg[:, GH:2 * GH], in_=ssg[:, 0:GH], func=AF.Sqrt,
                                                           scale=1.0 / 512.0, bias=epsT[:, 0:1]),
                             reads=[t_ss, t_c], writes=[t_ss])
                        S.op('dve', lambda e: e.reciprocal(out=ssg[:, 2 * GH:3 * GH], in_=ssg[:, GH:2 * GH]),
                             reads=[t_ss], writes=[t_ss])
                        def f_on(e):
                            for h in range(GH):
                                ins = e.activation(out=onb[:, h * 512:(h + 1) * 512], in_=PS[4 + h][:, :512],
                                                   func=AF.Copy, scale=ssg[:, 2 * GH + h:2 * GH + h + 1])
                            return ins
                        S.op('act', f_on, reads=tPS[4:4 + GH] + [t_ss], writes=[t_on])
                        def f_ot(e):
                            for vc in range(VC):
                                ins = e.transpose(out=PSB[vc // 8][:, (vc % 8) * P:(vc % 8 + 1) * P],
                                                  in_=onb[:, vc * P:(vc + 1) * P], identity=identb)
                            return ins
                        S.op('pe', f_ot, reads=[t_on, t_c], writes=tPS[0:NVB])
                        osl = (ci // 1) % 2
                        def f_ogr(e):
                            for vc in range(VC):
                                ins = e.scalar_tensor_tensor(out=ogr[osl][:, vc, :],
                                                             in0=PSB[vc // 8][:, (vc % 8) * P:(vc % 8 + 1) * P],
                                                             scalar=gn_col[:, vc % 4:vc % 4 + 1], in1=rF[sl][:, vc, :],
                                                             op0=ALU.mult, op1=ALU.mult)
                            return ins
                        S.op('dve', f_ogr, reads=tPS[0:NVB] + [t_c, t_ld[sl]], writes=[t_ogr[osl]])
                        S.dma('pool', rows_view(s_ogr, 0, VC, tm, P), ogr[osl][:], reads=[t_ogr[osl]], semtok=t_ogr[osl])
                    for idx in range(QC):
                        h = idx // 2
                        bk = [2, 3, 0, 1][idx % 4]
                        S.op('pe', lambda e: e.matmul(PS[bk][:, :512], lhsT=koutb[:, idx * P:(idx + 1) * P],
                                                      rhs=vtok[:, h * 512:(h + 1) * 512], start=True, stop=True),
                             reads=[t_kout, t_vtok], writes=[tPS[bk]])
                        S.op('dve', lambda e: e.scalar_tensor_tensor(out=Sst[:, idx, :], in0=Sst[:, idx, :],
                                                                     scalar=ebs[:, idx, P - 1:P], in1=PS[bk][:, :512],
                                                                     op0=ALU.mult, op1=ALU.add),
                             reads=[tPS[bk], t_ebs], writes=[t_S[idx]])
                        S.op('pool', lambda e: e.tensor_copy(out=Sbf[:, idx, :], in_=Sst[:, idx, :]),
                             reads=[t_S[idx]], writes=[t_Sbf[idx]])
                S.end()


        s_mg = dscr("s_mg", [D, N], BF16)
        TGm = min(1024, N)
        NGm = N // TGm
        NH4 = (TGm + 511) // 512
        hw4 = min(512, TGm)
        if cfg.stages >= 4:
            with ExitStack() as st:
                S.begin()
                lsb = lambda nm, shape, dt=F32: st.enter_context(nc.sbuf_tensor(un(nm), list(shape), dt))
                brs = [(w_br_gla, s_ogr, GV // P), (w_br_conv, s_zc, CW // P), (w_br_xa, s_ox, XW // P)]
                MXC = max(b[2] for b in brs)
                JW = 2
                bin_ = [lsb("m_in", [P, b[2], TGm], BF16) for b in brs]
                NW = 2
                wst = [lsb("m_wst", [P, MXC, JW * P]) for _ in range(NW)]
                wbf = [lsb("m_wbf", [P, MXC, JW * P], BF16) for _ in range(NW)]
                gt = [lsb("m_gt", [P, 3, TGm]) for _ in range(2)]
                mt = [[lsb("m_mt", [P, TGm]) for _ in range(3)] for _ in range(JW)]
                mo = [lsb("m_mo", [P, TGm], BF16) for _ in range(2)]
                t_in = S.tok("min")
                t_wst, t_wbf = S.toks("mwst", NW), S.toks("mwbf", NW)
                t_gt, t_mo = S.toks("mgt", 2), S.toks("mmo", 2)
                t_mt = [S.toks("mmt%d" % i, 3) for i in range(JW)]
                pq = 0
                for g in range(NGm):
                    for bi, (wb, sc_, nck) in enumerate(brs):
                        S.dma('sp', bin_[bi][:], rows_view(sc_, 0, nck, g * TGm, TGm), writes=[t_in], semtok=t_in)
                    tiles = [(jp, bi) for jp in range(KC // JW) for bi in range(3)]

                    def ldw(ti):
                        jp, bi = tiles[ti]
                        wb, sc_, nck = brs[bi]
                        sl = ti % NW
                        S.dma('sp', wst[sl][:, :nck, :], rows_view(wb, 0, nck, jp * JW * P, JW * P), writes=[t_wst[sl]],
                              semtok=t_wst[sl])
                    for ti in range(min(NW - 1, len(tiles))):
                        ldw(ti)
                    for ti, (jp, bi) in enumerate(tiles):
                        if ti + NW - 1 < len(tiles):
                            ldw(ti + NW - 1)
                        wb, sc_, nck = brs[bi]
                        sl = ti % NW
                        if bi == 0:
                            for jj in range(JW):
                                j = jp * JW + jj
                                for b3 in range(3):
                                    S.dma('sp', gt[j % 2][:, b3, :],
                                          s_g[b3 * D + j * P:b3 * D + (j + 1) * P, g * TGm:(g + 1) * TGm],
                                          writes=[t_gt[j % 2]], semtok=t_gt[j % 2])
                        S.op('pool' if ti % 2 else 'dve',
                             lambda e: e.tensor_copy(out=wbf[sl][:, :nck, :], in_=wst[sl][:, :nck, :]),
                             reads=[t_wst[sl]], writes=[t_wbf[sl]])
                        for jj in range(JW):
                            j = jp * JW + jj
                            gsl = j % 2
                            pb = (pq % (8 // NH4)) * NH4
                            pq += 1

                            def mm(e):
                                for hf in range(NH4):
                                    for c in range(nck):
                                        ins = e.matmul(PS[pb + hf][:, :hw4], lhsT=wbf[sl][:, c, jj * P:(jj + 1) * P],
                                                       rhs=bin_[bi][:, c, hf * hw4:(hf + 1) * hw4],
                                                       start=(c == 0), stop=(c == nck - 1))
                                return ins
                            S.op('pe', mm, reads=[t_wbf[sl], t_in], writes=tPS[pb:pb + NH4])

                            def gm(e):
                                for hf in range(NH4):
                                    ins = e.tensor_tensor(out=mt[jj][bi][:, hf * hw4:(hf + 1) * hw4],
                                                          in0=PS[pb + hf][:, :hw4],
                                                          in1=gt[gsl][:, bi, hf * hw4:(hf + 1) * hw4], op=ALU.mult)
                                return ins
                            S.op('dve', gm, reads=tPS[pb:pb + NH4] + [t_gt[gsl]], writes=[t_mt[jj][bi]])
                            if bi == 2:
                                S.op('pool', lambda e: e.tensor_tensor(out=mt[jj][0][:], in0=mt[jj][0][:], in1=mt[jj][1][:],
                                                                       op=ALU.add),
                                     reads=[t_mt[jj][1]], writes=[t_mt[jj][0]])
                                S.op('pool', lambda e: e.tensor_tensor(out=mo[gsl][:], in0=mt[jj][0][:], in1=mt[jj][2][:],
                                                                       op=ALU.add),
                                     reads=[t_mt[jj][0], t_mt[jj][2]], writes=[t_mo[gsl]])
                                S.dma('pool', s_mg[j * P:(j + 1) * P, g * TGm:(g + 1) * TGm], mo[gsl][:],
                                      reads=[t_mo[gsl]], semtok=t_mo[gsl])
                S.end()

        if cfg.stages >= 4:
            with ExitStack() as st:
                S.begin()
                lsb = lambda nm, shape, dt=F32: st.enter_context(nc.sbuf_tensor(un(nm), list(shape), dt))
                cgw = min(512, D)
                qw4 = min(256, cgw)
                nq = cgw // qw4
                KHo = min(KC, 16)
                nkh = KC // KHo
                mg = lsb("o_mg", [P, KC, TGm], BF16)
                wst = [lsb("o_wst", [P, KHo, qw4]) for _ in range(2)]
                wbf = [lsb("o_wbf", [P, KC, cgw], BF16) for _ in range(2)]
                xt = [lsb("o_xt", [P, cgw]) for _ in range(2)]
                ot = [lsb("o_ot", [P, cgw]) for _ in range(2)]
                t_mg = S.tok("omg")
                t_wst, t_wbf, t_xt, t_ot = S.toks("owst", 2), S.toks("owbf", 2), S.toks("oxt", 2), S.toks("oot", 2)
                k = 0
                wi = 0
                for g in range(NGm):
                    S.dma('sp', mg[:], rows_view(s_mg, 0, KC, g * TGm, TGm), writes=[t_mg], semtok=t_mg)
                    for cg in range(D // cgw):
                        bsl = (g * (D // cgw) + cg) % 2
                        for q in range(nq):
                            for kh in range(nkh):
                                sl = wi % 2
                                wi += 1
                                S.dma('sp', wst[sl][:], rows_view(w_o, kh * KHo * P, KHo, cg * cgw + q * qw4, qw4),
                                      writes=[t_wst[sl]], semtok=t_wst[sl])
                                S.op('pool' if wi % 2 else 'dve',
                                     lambda e: e.tensor_copy(out=wbf[bsl][:, kh * KHo:(kh + 1) * KHo, q * qw4:(q + 1) * qw4],
                                                             in_=wst[sl][:]),
                                     reads=[t_wst[sl]], writes=[t_wbf[bsl]])
                        for tt in range(TGm // P):
                            sl = k % 2
                            pb = k % 4
                            k += 1
                            r0 = g * TGm + tt * P
                            S.dma('sp', xt[sl][:], x_main[r0:r0 + P, cg * cgw:(cg + 1) * cgw], writes=[t_xt[sl]],
                                  semtok=t_xt[sl])

                            def mm(e):
                                for c in range(KC):
                                    ins = e.matmul(PS[pb][:, :cgw], lhsT=mg[:, c, tt * P:(tt + 1) * P], rhs=wbf[bsl][:, c, :],
                                                   start=(c == 0), stop=(c == KC - 1))
                                return ins
                            S.op('pe', mm, reads=[t_mg, t_wbf[bsl]], writes=[tPS[pb]])
                            S.op('dve', lambda e: e.tensor_tensor(out=ot[sl][:], in0=PS[pb][:, :cgw], in1=xt[sl][:],
                                                                  op=ALU.add),
                                 reads=[tPS[pb], t_xt[sl]], writes=[t_ot[sl]])
                            S.dma('pool', s_x2[r0:r0 + P, cg * cgw:(cg + 1) * cgw], ot[sl][:], reads=[t_ot[sl]],
                                  semtok=t_ot[sl])
                S.end()

        NTT = N // P
        HC = 2 * PH
        s_hn2 = dscr("s_hn2", [D, N], BF16)
        s_gg = dscr("s_gg", [NE, N], BF16)
        if cfg.stages >= 5:
            with ExitStack() as st:
                actT2 = st.enter_context(nc.sbuf_tensor(un("actT2"), [P, KC, N], BF16))
                t_act2 = S.tok("actT2")
                norm_T(s_x2, norm_ffn, actT2, t_act2, N)
                S.dma('sp', rows_view(s_hn2, 0, KC, 0, N), actT2[:], reads=[t_act2], semtok=t_act2)
                bl = blocks_for(0, PQ, 'copy', lambda r, n: s_qp[r:r + n, 0:N], BF16)
                proj(peer_wq, KC, actT2, t_act2, N, 0, bl, "pwq")
                S.barrier()

        if cfg.stages >= 5:
            with ExitStack() as st:
                S.begin()
                lsb = lambda nm, shape, dt=F32: st.enter_context(nc.sbuf_tensor(un(nm), list(shape), dt))
                NSB = (HC * P + 511) // 512
                assert NSB <= 4
                EB = 16
                NEB = P // EB
                NEC = NE // P
                HT = min(1024, N)
                NPASS = N // HT
                NGh = HT // TG
                skT = lsb("p_skT", [P, HC, P], BF16)
                sktmp = [lsb("p_sktmp", [P, P]) for _ in range(2)]
                qp = [lsb("p_qp", [P, HC, P], BF16) for _ in range(2)]
                ssb = lsb("p_ssb", [P, HC, P])
                swk = lsb("p_swk", [P, HC, P])
                sm = lsb("p_sm", [P, HC, P])
                sv = lsb("p_sv", [P, HC, 16])
                cand = lsb("p_cand", [P, PH, 256])
                c8 = lsb("p_c8", [P, PH, 24])
                sml = lsb("p_sml", [P, 6, PH])
                j16 = lsb("p_j16", [P, 16])
                NSL = 3
                Dt = [lsb("p_D", [P, EB, P]) for _ in range(NSL)]
                Et = [lsb("p_E", [P, EB, P], BF16) for _ in range(NSL)]
                Wh = [lsb("p_Wh", [P, EB * P], BF16) for _ in range(NSL)]
                wtT = [lsb("p_wtT", [P, EB, P], BF16) for _ in range(2)]
                t_skT = S.tok("pskT")
                t_sktmp, t_qp = S.toks("psktmp", 2), S.toks("pqp", 2)
                t_ssb, t_swk, t_sm, t_sv = S.tok("pssb"), S.tok("pswk"), S.tok("psm"), S.tok("psv")
                t_cand, t_c8, t_sml, t_j16 = S.tok("pcand"), S.tok("pc8"), S.tok("psml"), S.tok("pj16")
                t_D, t_E, t_Wh, t_wtT = S.toks("pD", NSL), S.toks("pE", NSL), S.toks("pWh", NSL), S.toks("pwtT", 2)
                HD = max(D // 4, P)
                NHf = D // HD
                KH2 = HD // P
                actT2h = lsb("u_act", [P, KC, HT], BF16)
                ust = [lsb("u_st", [P, HD]) for _ in range(2)]
                UT = [lsb("u_T", [P, KC, P], BF16) for _ in range(2)]
                Gs = [lsb("u_G", [P, HT], BF16) for _ in range(2)]
                t_acth = S.tok("uacth")
                t_ust, t_UT, t_G = S.toks("uust", 2), S.toks("uUT", 2), S.toks("uG", 2)
                for hc in range(HC):
                    sl = hc % 2
                    S.dma('sp', sktmp[sl][:], peer_sk[hc * P:(hc + 1) * P, :], writes=[t_sktmp[sl]], semtok=t_sktmp[sl])
                    S.op('pe', lambda e: e.transpose(out=PS[sl][:, :P], in_=sktmp[sl][:], identity=identf),
                         reads=[t_sktmp[sl], t_c], writes=[tPS[sl]])
                    S.op('dve', lambda e: e.tensor_copy(out=skT[:, hc, :], in_=PS[sl][:, :P]), reads=[tPS[sl]],
                         writes=[t_skT])

                def each(n, fn):
                    def g(e):
                        for i in range(n):
                            ins = fn(e, i)
                        return ins
                    return g

                def ldq(ti):
                    S.dma('sp', qp[ti % 2][:], rows_view(s_qp, 0, HC, ti * P, P), writes=[t_qp[ti % 2]],
                          semtok=t_qp[ti % 2])
                wkc = [0]

                def route_tile(ti):
                    sl = ti % 2
                    if ti + 1 < NTT:
                        ldq(ti + 1)

                    def f_sc(e):
                        for hc in range(HC):
                            ins = e.matmul(PS[hc // 4][:, (hc % 4) * P:(hc % 4 + 1) * P], lhsT=qp[sl][:, hc, :],
                                           rhs=skT[:, hc, :], start=True, stop=True)
                        return ins
                    S.op('pe', f_sc, reads=[t_qp[sl], t_skT], writes=tPS[0:NSB])

                    def f_cp(e):
                        for b in range(NSB):
                            n4 = min(4, HC - 4 * b)
                            ins = e.copy(out=ssb[:, 4 * b:4 * b + n4, :],
                                         in_=PS[b][:, :n4 * P].rearrange("p (c n) -> p c n", n=P))
                        return ins
                    S.op('act', f_cp, reads=tPS[0:NSB], writes=[t_ssb])
                    S.op('dve', each(HC, lambda e, hc: e.max(out=sv[:, hc, 0:8], in_=ssb[:, hc, :])),
                         reads=[t_ssb], writes=[t_sv])
                    S.op('dve', each(HC, lambda e, hc: e.match_replace(out=swk[:, hc, :], in_to_replace=sv[:, hc, 0:8],
                                                                       in_values=ssb[:, hc, :], imm_value=-1.0e30)),
                         reads=[t_ssb, t_sv], writes=[t_swk])
                    S.op('dve', each(HC, lambda e, hc: e.max(out=sv[:, hc, 8:16], in_=swk[:, hc, :])),
                         reads=[t_swk], writes=[t_sv])
                    S.op('dve', each(HC, lambda e, hc: e.tensor_scalar(out=swk[:, hc, :], in0=ssb[:, hc, :],
                                                                       scalar1=sv[:, hc, 15:16], scalar2=1.0,
                                                                       op0=ALU.is_ge, op1=ALU.subtract)),
                         reads=[t_ssb, t_sv], writes=[t_swk])
                    S.op('dve', each(HC, lambda e, hc: e.scalar_tensor_tensor(out=sm[:, hc, :], in0=swk[:, hc, :],
                                                                              scalar=NEG_BIG, in1=ssb[:, hc, :],
                                                                              op0=ALU.mult, op1=ALU.add)),
                         reads=[t_swk, t_ssb], writes=[t_sm])
                    S.op('dve', each(PH, lambda e, h: e.tensor_tensor(
                        out=cand[:, h, :].rearrange("p (i j) -> p i j", j=16),
                        in0=sv[:, 2 * h, :].unsqueeze(2).to_broadcast([P, 16, 16]),
                        in1=sv[:, 2 * h + 1, :].unsqueeze(1).to_broadcast([P, 16, 16]), op=ALU.add)),
                         reads=[t_sv], writes=[t_cand])
                    for rnd in range(3):
                        S.op('dve', each(PH, lambda e, h: e.max(out=c8[:, h, 8 * rnd:8 * rnd + 8], in_=cand[:, h, :])),
                             reads=[t_cand], writes=[t_c8])
                        if rnd < 2:
                            S.op('dve', each(PH, lambda e, h: e.match_replace(
                                out=cand[:, h, :], in_to_replace=c8[:, h, 8 * rnd:8 * rnd + 8], in_values=cand[:, h, :],
                                imm_value=-1.0e30)), reads=[t_c8], writes=[t_cand])
                    S.op('dve', lambda e: e.tensor_tensor(out=sml[:, 0, :], in0=c8[:, :, 15], in1=c8[:, :, 16], op=ALU.add),
                         reads=[t_c8], writes=[t_sml])
                    S.op('dve', lambda e: e.tensor_scalar(out=sml[:, 0, :], in0=sml[:, 0, :], scalar1=0.5, scalar2=None,
                                                          op0=ALU.mult), reads=[t_sml], writes=[t_sml])
                    S.op('dve', lambda e: e.tensor_scalar(out=sml[:, 1, :], in0=c8[:, :, 0], scalar1=-1.0, scalar2=None,
                                                          op0=ALU.mult), reads=[t_c8], writes=[t_sml])
                    S.op('act', each(PH, lambda e, h: e.activation(out=j16[:], in_=c8[:, h, 0:16], func=AF.Exp,
                                                                   bias=sml[:, 1, h:h + 1],
                                                                   accum_out=sml[:, 2, h:h + 1])),
                         reads=[t_c8, t_sml], writes=[t_j16, t_sml])
                    S.op('act', lambda e: e.activation(out=sml[:, 3, :], in_=sml[:, 2, :], func=AF.Ln),
                         reads=[t_sml], writes=[t_sml])
                    S.op('dve', lambda e: e.tensor_tensor(out=sml[:, 4, :], in0=sml[:, 1, :], in1=sml[:, 3, :],
                                                          op=ALU.subtract),
                         reads=[t_sml], writes=[t_sml])
                    for eb in range(NEB):
                        wsl = eb % 2
                        for h in range(PH):
                            ds = wkc[0] % NSL
                            wkc[0] += 1
                            S.op('pool', lambda e: e.tensor_tensor(
                                out=Dt[ds][:],
                                in0=sm[:, 2 * h, eb * EB:(eb + 1) * EB].unsqueeze(2).to_broadcast([P, EB, P]),
                                in1=sm[:, 2 * h + 1, :].unsqueeze(1).to_broadcast([P, EB, P]), op=ALU.add),
                                 reads=[t_sm], writes=[t_D[ds]])
                            S.op('act', lambda e: e.activation(out=Et[ds][:], in_=Dt[ds][:], func=AF.Exp,
                                                               bias=sml[:, 4, h:h + 1]),
                                 reads=[t_D[ds], t_sml], writes=[t_E[ds]])
                            S.op('dve', lambda e: e.scalar_tensor_tensor(
                                out=Wh[ds][:], in0=Dt[ds][:].rearrange("p a b -> p (a b)"), scalar=sml[:, 0, h:h + 1],
                                in1=Et[ds][:].rearrange("p a b -> p (a b)"), op0=ALU.is_ge, op1=ALU.mult),
                                 reads=[t_D[ds], t_E[ds], t_sml], writes=[t_Wh[ds]])

                            def f_tr(e):
                                for c in range(EB):
                                    ins = e.matmul(PS[c // 4][:, (c % 4) * P:(c % 4 + 1) * P],
                                                   lhsT=Wh[ds][:, c * P:(c + 1) * P], rhs=identb,
                                                   start=(h == 0 and c % 4 == 0), stop=(h == PH - 1 and c % 4 == 3),
                                                   skip_group_check=True)
                                return ins
                            S.op('pe', f_tr, reads=[t_Wh[ds], t_c], writes=tPS[0:4])

                        def f_ev(e):
                            for b in range(4):
                                ins = e.copy(out=wtT[wsl][:, 4 * b:4 * b + 4, :],
                                             in_=PS[b][:, :].rearrange("p (c n) -> p c n", n=P))
                            return ins
                        S.op('act', f_ev, reads=tPS[0:4], writes=[t_wtT[wsl]])
                        S.dma('act', rows_view(s_wt, eb * EB * P, EB, ti * P, P), wtT[wsl][:], reads=[t_wtT[wsl]],
                              semtok=t_wtT[wsl])
                        yield

                ujobs = [(p_, c, hf) for p_ in range(NPASS) for c in range(NEC) for hf in range(NHf)]
                ujn = [0]
                tbn = [0]

                def ldu(k):
                    p_, c, hf = ujobs[k]
                    S.dma('sp', ust[k % 2][:], peer_u[c * P:(c + 1) * P, hf * HD:(hf + 1) * HD], writes=[t_ust[k % 2]],
                          semtok=t_ust[k % 2])

                def act_chunk(p_, c):
                    usl = c % 2
                    for hf in range(NHf):
                        k = ujn[0]
                        ujn[0] += 1
                        assert ujobs[k] == (p_, c, hf)
                        sl = k % 2
                        if k + 1 < len(ujobs):
                            ldu(k + 1)
                        for k4 in range(0, KH2, 4):
                            n4 = min(4, KH2 - k4)
                            bk = 4 + tbn[0] % 2
                            tbn[0] += 1

                            def f_tr(e):
                                for i in range(n4):
                                    ins = e.transpose(out=PS[bk][:, i * P:(i + 1) * P],
                                                      in_=ust[sl][:, (k4 + i) * P:(k4 + i + 1) * P], identity=identf)
                                return ins
                            S.op('pe', f_tr, reads=[t_ust[sl], t_c], writes=[tPS[bk]])
                            dstu = UT[usl][:, hf * KH2 + k4:hf * KH2 + k4 + n4, :]
                            srcu = PS[bk][:, :n4 * P].rearrange("p (c n) -> p c n", n=P)
                            S.op('act', lambda e: e.copy(out=dstu, in_=srcu), reads=[tPS[bk]], writes=[t_UT[usl]])

                    def f_a(e):
                        for g in range(NGh):
                            for dc in range(KC):
                                ins = e.matmul(PS[6 + g][:, :TG], lhsT=UT[usl][:, dc, :],
                                               rhs=actT2h[:, dc, g * TG:(g + 1) * TG], start=(dc == 0), stop=(dc == KC - 1))
                        return ins
                    S.op('pe', f_a, reads=[t_UT[usl], t_acth], writes=tPS[6:6 + NGh])

                    def f_g(e):
                        for g in range(NGh):
                            ins = e.activation(out=Gs[usl][:, g * TG:(g + 1) * TG], in_=PS[6 + g][:, :TG], func=AF.Gelu)
                        return ins
                    S.op('act', f_g, reads=tPS[6:6 + NGh], writes=[t_G[usl]])
                    S.dma('act', s_gg[c * P:(c + 1) * P, p_ * HT:(p_ + 1) * HT], Gs[usl][:], reads=[t_G[usl]],
                          semtok=t_G[usl])

                ldq(0)
                ldu(0)
                TPP = HT // P
                for p_ in range(NPASS):
                    S.dma('sp', actT2h[:], rows_view(s_hn2, 0, KC, p_ * HT, HT), writes=[t_acth], semtok=t_acth)
                    nsteps = TPP * NEB
                    per = (NEC + nsteps - 1) // nsteps
                    cn = 0
                    for ti in range(p_ * TPP, (p_ + 1) * TPP):
                        for _ in route_tile(ti):
                            for _k in range(per):
                                if cn < NEC:
                                    act_chunk(p_, cn)
                                    cn += 1
                    while cn < NEC:
                        act_chunk(p_, cn)
                        cn += 1
                S.end()

        if cfg.stages >= 5:
            with ExitStack() as st:
                S.begin()
                lsb = lambda nm, shape, dt=F32: st.enter_context(nc.sbuf_tensor(un(nm), list(shape), dt))
                NB5 = 3
                NEC = NE // P
                wa = [lsb("w_a", [P, N], BF16) for _ in range(NB5)]
                wb_ = [lsb("w_b", [P, N], BF16) for _ in range(NB5)]
                wc = [lsb("w_c", [P, N], BF16) for _ in range(NB5)]
                t_wa, t_wc = S.toks("wwa", NB5), S.toks("wwc", NB5)

                def ldw5(c):
                    sl = c % NB5
                    S.dma('sp', wa[sl][:], s_wt[c * P:(c + 1) * P, 0:N], writes=[t_wa[sl]], semtok=t_wa[sl])
                    S.dma('sp', wb_[sl][:], s_gg[c * P:(c + 1) * P, 0:N], writes=[t_wa[sl]], semtok=t_wa[sl])
                for c in range(min(NB5 - 1, NEC)):
                    ldw5(c)
                for c in range(NEC):
                    if c + NB5 - 1 < NEC:
                        ldw5(c + NB5 - 1)
                    sl = c % NB5
                    S.op('dve' if c % 2 else 'pool',
                         lambda e: e.tensor_tensor(out=wc[sl][:], in0=wa[sl][:], in1=wb_[sl][:], op=ALU.mult),
                         reads=[t_wa[sl]], writes=[t_wc[sl]])
                    S.dma('act', s_wg[c * P:(c + 1) * P, 0:N], wc[sl][:], reads=[t_wc[sl]], semtok=t_wc[sl])
                S.end()

        if cfg.stages >= 5:
            with ExitStack() as st:
                S.begin()
                lsb = lambda nm, shape, dt=F32: st.enter_context(nc.sbuf_tensor(un(nm), list(shape), dt))
                cgw = min(512, D)
                NB3 = 3
                PT = min(8, NTT)
                vst = [lsb("v_st", [P, cgw]) for _ in range(NB3)]
                vb = [lsb("v_b", [P, cgw], BF16) for _ in range(NB3)]
                wgt = [lsb("v_wg", [P, PT * P], BF16) for _ in range(NB3)]
                xt = [lsb("v_xt", [P, cgw]) for _ in range(2)]
                ot = [lsb("v_ot", [P, cgw]) for _ in range(2)]
                t_vst, t_vb, t_wgt = S.toks("vst", NB3), S.toks("vb", NB3), S.toks("vwg", NB3)
                t_xt, t_ot = S.toks("vxt", 2), S.toks("vot", 2)
                NEC = NE // P
                k = 0
                for pp in range(NTT // PT):
                    for cg in range(D // cgw):
                        def ld(ec):
                            sl = ec % NB3
                            S.dma('sp', vst[sl][:], peer_v[ec * P:(ec + 1) * P, cg * cgw:(cg + 1) * cgw], writes=[t_vst[sl]],
                                  semtok=t_vst[sl])
                            S.dma('sp', wgt[sl][:], s_wg[ec * P:(ec + 1) * P, pp * PT * P:(pp + 1) * PT * P],
                                  writes=[t_wgt[sl]], semtok=t_wgt[sl])
                        for ec in range(NB3 - 1):
                            ld(ec)
                        for ec in range(NEC):
                            if ec + NB3 - 1 < NEC:
                                ld(ec + NB3 - 1)
                            sl = ec % NB3
                            S.op('dve' if ec % 2 else 'pool', lambda e: e.tensor_copy(out=vb[sl][:], in_=vst[sl][:]),
                                 reads=[t_vst[sl]], writes=[t_vb[sl]])

                            def mm(e):
                                for tt in range(PT):
                                    ins = e.matmul(PS[tt][:, :cgw], lhsT=wgt[sl][:, tt * P:(tt + 1) * P], rhs=vb[sl][:],
                                                   start=(ec == 0), stop=(ec == NEC - 1))
                                return ins
                            S.op('pe', mm, reads=[t_vb[sl], t_wgt[sl]], writes=tPS[0:PT])
                        for tt in range(PT):
                            sl = k % 2
                            k += 1
                            r0 = (pp * PT + tt) * P
                            S.dma('sp', xt[sl][:], s_x2[r0:r0 + P, cg * cgw:(cg + 1) * cgw], writes=[t_xt[sl]],
                                  semtok=t_xt[sl])
                            S.op('dve', lambda e: e.tensor_tensor(out=ot[sl][:], in0=PS[tt][:, :cgw], in1=xt[sl][:],
                                                                  op=ALU.add),
                                 reads=[tPS[tt], t_xt[sl]], writes=[t_ot[sl]])
                            S.dma('pool', s_x3[r0:r0 + P, cg * cgw:(cg + 1) * cgw], ot[sl][:], reads=[t_ot[sl]],
                                  semtok=t_ot[sl])
                S.end()

        if cfg.stages >= 6:
            with ExitStack() as st:
                S.begin()
                lsb = lambda nm, shape, dt=F32: st.enter_context(nc.sbuf_tensor(un(nm), list(shape), dt))
                gb = lsb("f_gb", [P, D])
                xs = [lsb("f_xs", [P, D]) for _ in range(2)]
                ob = [lsb("f_ob", [P, D]) for _ in range(2)]
                junk = lsb("f_junk", [P, D], BF16)
                ss = lsb("f_ss", [P, 4])
                t_gb, t_junk, t_ss = S.tok("fgb"), S.tok("fjunk"), S.tok("fss")
                t_xs, t_ob = S.toks("fxs", 2), S.toks("fob", 2)
                S.dma('sp', gb[:], final_norm[0:1, :].to_broadcast([P, D]), writes=[t_gb], semtok=t_gb)
                S.dma('sp', xs[0][:], s_x3[0:P, :], writes=[t_xs[0]], semtok=t_xs[0])
                for i in range(NTT):
                    sl = i % 2
                    if i + 1 < NTT:
                        S.dma('sp', xs[1 - sl][:], s_x3[(i + 1) * P:(i + 2) * P, :], writes=[t_xs[1 - sl]],
                              semtok=t_xs[1 - sl])
                    S.op('act', lambda e: e.activation(out=junk[:], in_=xs[sl][:], func=AF.Square, accum_out=ss[:, 0:1]),
                         reads=[t_xs[sl]], writes=[t_junk, t_ss])
                    S.op('act', lambda e: e.activation(out=ss[:, 1:2], in_=ss[:, 0:1], func=AF.Sqrt, scale=1.0 / D,
                                                       bias=epsT[:, 0:1]), reads=[t_ss, t_c], writes=[t_ss])
                    S.op('dve', lambda e: e.reciprocal(out=ss[:, 2:3], in_=ss[:, 1:2]), reads=[t_ss], writes=[t_ss])
                    S.op('dve', lambda e: e.scalar_tensor_tensor(out=ob[sl][:], in0=xs[sl][:], scalar=ss[:, 2:3], in1=gb[:],
                                                                 op0=ALU.mult, op1=ALU.mult),
                         reads=[t_xs[sl], t_ss, t_gb], writes=[t_ob[sl]])
                    S.dma('pool', out[i * P:(i + 1) * P, :], ob[sl][:], reads=[t_ob[sl]], semtok=t_ob[sl])
                S.end()

        S.finish()
    return nc


def make_consts():
    s = np.arange(P)[:, None]
    t = np.arange(P)[None, :]
    ident = (s == t).astype(np.float32)
    triU = (s <= t).astype(np.float32) * (-1.0 / 16.0)
    triL = (s > t).astype(np.float32) * (-1.0 / 16.0)
    mask = (s <= t).astype(np.float32)
    ones = np.ones((P, P), np.float32)
    return np.ascontiguousarray(np.concatenate([ident, triU, triL, mask, ones], axis=1))


def run(cfg, inp, dbg=()):
    N, NPRE, D = cfg.N, cfg.NPRE, cfg.D
    x = np.asarray(inp["x"], np.float32)
    B, S, _ = x.shape
    halves = S // N
    ncores = B * halves
    assert ncores == 8
    f = lambda a: np.ascontiguousarray(np.asarray(a, np.float32))
    shared = {
        "norm_mix": f(inp["norm_mix"]).reshape(1, D),
        "w_in": f(inp["w_in"][0]),
        "w_aug": f(np.concatenate([np.asarray(inp["w_a_up"][0]), np.asarray(inp["b_a"][0])[None, :]], axis=0)),
        "gla_norm": f(inp["gla_norm"][0]).reshape(4, P),
        "conv_w": f(inp["conv_w"][0]).reshape(3 * cfg.CW // P, P),
        "w_br_gla": f(inp["w_br_gla"][0]),
        "w_br_conv": f(inp["w_br_conv"][0]),
        "w_mem_kv": f(inp["w_mem_kv"][0]),
        "w_br_xa": f(inp["w_br_xa"][0]),
        "b_gate": f(inp["b_gate"][0]).reshape(3 * D // P, P),
        "w_o": f(inp["w_o"][0]),
        "mem_norm": f(inp["mem_norm"]).reshape(1, D),
        "norm_ffn": f(inp["norm_ffn"]).reshape(1, D),
        "peer_wq": f(inp["peer_wq"][0]),
        "peer_sk": f(inp["peer_subkeys"][0]).reshape(2 * cfg.PH * P, P),
        "peer_u": f(inp["peer_u"][0]),
        "peer_v": f(inp["peer_v"][0]),
        "final_norm": f(inp["final_norm"]).reshape(1, D),
        "cst": make_consts(),
    }
    zeros_pre = np.zeros((NPRE, D), np.float32)
    in_maps = []
    for c in range(ncores):
        b, h = c // halves, c % halves
        m = dict(shared)
        m["x_main"] = f(x[b, h * N:(h + 1) * N])
        m["x_pre"] = f(x[b, h * N - NPRE:h * N]) if h > 0 else zeros_pre
        m["mem"] = f(inp["mem"][b])
        in_maps.append(m)
    nc = build(cfg, dbg=dbg)
    res = run_bass_kernel_spmd(nc, in_maps, core_ids=list(range(ncores)))
    return res.results


def kernel(**inputs):
    cfg = Cfg()
    r = run(cfg, inputs)
    x = inputs["x"]
    B, S, D = x.shape
    out = np.empty((B, S, D), np.float32)
    halves = S // cfg.N
    for c in range(B * halves):
        b, h = c // halves, c % halves
        out[b, h * cfg.N:(h + 1) * cfg.N] = r[c]["out"]
    return out
```

```python
import numpy as np
from contextlib import ExitStack
import concourse.bass as bass
import concourse.mybir as mybir
from concourse.bass_utils import run_bass_kernel_spmd

F32 = mybir.dt.float32
BF16 = mybir.dt.bfloat16
AF = mybir.ActivationFunctionType
ALU = mybir.AluOpType
P = 128
EPS = 1e-6
NEG_BIG = 1.0e4


class Cfg:
    def __init__(s, D=4096, N=2048, NPRE=2048, NMEM=256, GH=4, CW=2048, XH=4, PH=8, stages=99):
        s.D, s.N, s.NPRE, s.NMEM = D, N, NPRE, NMEM
        s.GH, s.DK, s.DV, s.RANK = GH, 256, 512, 16
        s.QK, s.GV = GH * 256, GH * 512
        s.CW, s.XH, s.DH, s.XW = CW, XH, 512, XH * 512
        s.PH, s.NK, s.PQ, s.NE = PH, 128, PH * 256, 128 * 128
        s.KC = D // P
        s.TG = min(512, N)
        s.HALO = min(512, NPRE)
        s.NT = NPRE + N
        sizes = [s.QK, s.QK, s.GV, s.GV, 16, CW, CW, CW, s.XW, 3 * D]
        s.off = [0]
        for z in sizes:
            s.off.append(s.off[-1] + z)
        s.INW = s.off[-1]
        s.stages = stages


class Tok:
    def __init__(self, name):
        self.name = name
        self.w = None
        self.r = {}
        self.sem = None
        self.total = 0
        self.dw = {}
        self.dr = {}


class Sched:
    ENG = ['pe', 'act', 'dve', 'pool', 'sp']

    def __init__(self, nc, stack):
        self.nc = nc
        self.stack = stack
        self.eng = {'pe': nc.tensor, 'act': nc.scalar, 'dve': nc.vector, 'pool': nc.gpsimd, 'sp': nc.sync}
        self.sem, self.tick, self.seen, self.seen_d = {}, {}, {}, {}
        self.semtoks = []
        self.free = []
        self.scopes = []
        for e in self.ENG:
            self.sem[e] = stack.enter_context(nc.semaphore('s_' + e))
            self.tick[e] = 0
            self.seen[e] = {}
            self.seen_d[e] = {}

    def tok(self, name):
        t = Tok(name)
        if self.scopes:
            self.scopes[-1].append(t)
        return t

    def toks(self, name, n):
        return [self.tok('%s%d' % (name, i)) for i in range(n)]

    def begin(self):
        self.scopes.append([])

    def end(self):
        self.barrier()
        for t in self.scopes.pop():
            if t.sem is not None:
                self.free.append((t.sem, t.total))
                self.semtoks.remove(t)
                t.sem = None

    def _wait_eng(self, e, oe, t):
        if t <= 0 or self.seen[e].get(oe, 0) >= t:
            return
        self.eng[e].wait_ge(self.sem[oe], t)
        self.seen[e][oe] = t

    def _wait_dma(self, e, st):
        if st.total <= 0 or self.seen_d[e].get(st, 0) >= st.total:
            return
        self.eng[e].wait_ge(st.sem, st.total)
        self.seen_d[e][st] = st.total

    def _need(self, e, reads, writes):
        for t in reads:
            if t.w is not None:
                self._wait_eng(e, t.w[0], t.w[1])
            for st in t.dw:
                self._wait_dma(e, st)
        for t in writes:
            if t.w is not None:
                self._wait_eng(e, t.w[0], t.w[1])
            for oe, tk in t.r.items():
                self._wait_eng(e, oe, tk)
            for st in t.dw:
                self._wait_dma(e, st)
            for st in t.dr:
                self._wait_dma(e, st)

    def op(self, e, fn, reads=(), writes=()):
        self._need(e, reads, writes)
        ins = fn(self.eng[e])
        self.tick[e] += 1
        ins.then_inc(self.sem[e], 1)
        tk = self.tick[e]
        for t in reads:
            t.r[e] = tk
        for t in writes:
            t.w = (e, tk)
            t.r = {}
            t.dw = {}
            t.dr = {}
        return ins

    def dma(self, q, out, in_, reads=(), writes=(), semtok=None, **kw):
        self._need(q, reads, writes)
        ins = self.eng[q].dma_start(out=out, in_=in_, **kw)
        st = semtok
        if st.sem is None:
            if self.free:
                st.sem, st.total = self.free.pop()
            else:
                self.nsem = getattr(self, 'nsem', 0) + 1
                st.sem = self.stack.enter_context(self.nc.semaphore('d%d' % self.nsem))
                st.total = 0
            self.semtoks.append(st)
        st.total += 16
        ins.then_inc(st.sem, 16)
        for t in reads:
            t.dr[st] = True
        for t in writes:
            t.dw[st] = True
            t.w = None
            t.r = {}
        return ins

    def barrier(self):
        for e in self.ENG:
            for oe in self.ENG:
                if oe != e:
                    self._wait_eng(e, oe, self.tick[oe])
            for st in self.semtoks:
                self._wait_dma(e, st)

    def finish(self):
        for st in self.semtoks:
            self._wait_dma('sp', st)
        for oe in self.ENG:
            if oe != 'sp':
                self._wait_eng('sp', oe, self.tick[oe])


def build(cfg, dbg=()):
    nc = bass.Bass("TRN2", target_bir_lowering=False)
    D, N, NPRE, NT, NMEM, KC, TG = cfg.D, cfg.N, cfg.NPRE, cfg.NT, cfg.NMEM, cfg.KC, cfg.TG
    QK, GV, CW, XW, PQ, NE, GH, PH, XH = cfg.QK, cfg.GV, cfg.CW, cfg.XW, cfg.PQ, cfg.NE, cfg.GH, cfg.PH, cfg.XH
    HALO = cfg.HALO
    NG = N // TG

    uid = [0]

    def un(name):
        uid[0] += 1
        return "%s_%d" % (name, uid[0])

    def din(name, shape, dt=F32):
        return nc.dram_tensor(name, list(shape), dt, kind="ExternalInput").ap()

    def dscr(name, shape, dt=F32):
        kind = "ExternalOutput" if name in dbg else "Internal"
        return nc.dram_tensor(name, list(shape), dt, kind=kind).ap()

    x_main = din("x_main", [N, D])
    x_pre = din("x_pre", [NPRE, D])
    mem = din("mem", [NMEM, D])
    norm_mix = din("norm_mix", [1, D])
    w_in = din("w_in", [D, cfg.INW])
    w_aug = din("w_aug", [17, QK])
    gla_norm = din("gla_norm", [4, P])
    conv_w = din("conv_w", [3 * CW // P, P])
    w_br_gla = din("w_br_gla", [GV, D])
    w_br_conv = din("w_br_conv", [CW, D])
    w_mem_kv = din("w_mem_kv", [D, 2 * XW])
    w_br_xa = din("w_br_xa", [XW, D])
    b_gate = din("b_gate", [3 * D // P, P])
    w_o = din("w_o", [D, D])
    mem_norm = din("mem_norm", [1, D])
    norm_ffn = din("norm_ffn", [1, D])
    peer_wq = din("peer_wq", [D, PQ])
    peer_sk = din("peer_sk", [2 * PH * P, P])
    peer_u = din("peer_u", [NE, D])
    peer_v = din("peer_v", [NE, D])
    final_norm = din("final_norm", [1, D])
    cst = din("cst", [P, 5 * P])
    out = nc.dram_tensor("out", [N, D], F32, kind="ExternalOutput").ap()

    s_q = dscr("s_q", [QK, N])
    s_k = dscr("s_k", [QK, NT])
    s_v = dscr("s_v", [GV, NT], BF16)
    s_r = dscr("s_r", [GV, N])
    s_alr = dscr("s_alr", [16, NT])
    s_cb = dscr("s_cb", [CW, N])
    s_cc = dscr("s_cc", [CW, HALO + N])
    s_ch = dscr("s_ch", [CW, HALO + N])
    s_qx = dscr("s_qx", [XW, N], BF16)
    s_g = dscr("s_g", [3 * D, N])
    s_mkv = dscr("s_mkv", [2 * XW, NMEM], BF16)
    s_ogr = dscr("s_ogr", [GV, N], BF16)
    s_zc = dscr("s_zc", [CW, N], BF16)
    s_ox = dscr("s_ox", [XW, N], BF16)
    s_x2 = dscr("s_x2", [N, D])
    s_qp = dscr("s_qp", [PQ, N], BF16)
    s_wt = dscr("s_wt", [NE, N], BF16)
    s_wg = dscr("s_wg", [NE, N], BF16)
    s_x3 = dscr("s_x3", [N, D])

    with ExitStack() as st0:
        S = Sched(nc, st0)
        sb = lambda name, shape, dt=F32: st0.enter_context(nc.sbuf_tensor(name, list(shape), dt))
        PS = [st0.enter_context(nc.psum_tensor("ps%d" % i, [P, 512], F32)) for i in range(8)]
        tPS = S.toks("ps", 8)
        PSB = [p[:].bitcast(BF16) for p in PS]
        cf = sb("cf", [P, 5 * P])
        cb16 = sb("cb16", [P, 5 * P], BF16)
        epsT = sb("epsT", [P, 1])
        oneT = sb("oneT", [P, 1])
        t_c = S.tok("consts")
        S.dma('sp', cf[:], cst[:, :], writes=[t_c], semtok=t_c)
        S.op('dve', lambda e: e.tensor_copy(out=cb16[:], in_=cf[:]), reads=[t_c], writes=[t_c])
        S.op('dve', lambda e: e.memset(epsT[:], EPS), writes=[t_c])
        S.op('dve', lambda e: e.memset(oneT[:], 1.0), writes=[t_c])
        identf, triU, triL, mask01f, onesf = [cf[:, i * P:(i + 1) * P] for i in range(5)]
        identb, _, _, mask01b, onesb = [cb16[:, i * P:(i + 1) * P] for i in range(5)]

        def rows_view(ap2d, r0, nchunk, c0, ncol):
            return ap2d[r0:r0 + nchunk * P, c0:c0 + ncol].rearrange("(c p) n -> p c n", p=P)

        def load_cols(dst, src, n, name):
            with ExitStack() as st:
                S.begin()
                tmp = st.enter_context(nc.sbuf_tensor(un("lc_" + name), [P, P], F32))
                tt = S.tok("lc_" + name)
                S.dma('sp', tmp[:n, :], src[0:n, :], writes=[tt], semtok=tt)
                S.op('pe', lambda e: e.transpose(out=PS[0][:, :n], in_=tmp[:n, :], identity=identf[:n, :n]),
                     reads=[tt, t_c], writes=[tPS[0]])
                S.op('dve', lambda e: e.tensor_copy(out=dst, in_=PS[0][:, :n]), reads=[tPS[0]], writes=[t_c])
                S.end()

        def norm_T(src, gain_ap, actT, t_act, ntok, col0=0):
            with ExitStack() as st:
                S.begin()
                lsb = lambda name, shape, dt=F32: st.enter_context(nc.sbuf_tensor(un(name), list(shape), dt))
                gb = lsb("nt_gb", [P, D])
                xs = [lsb("nt_xs%d" % i, [P, D]) for i in range(2)]
                xb = lsb("nt_xb", [P, D], BF16)
                junk = lsb("nt_junk", [P, D], BF16)
                ss = lsb("nt_ss", [P, 4])
                t_gb, t_xb, t_junk, t_ss = S.tok("gb"), S.tok("xb"), S.tok("junk"), S.tok("ss")
                t_xs = S.toks("xs", 2)
                S.dma('sp', gb[:], gain_ap[0:1, :].to_broadcast([P, D]), writes=[t_gb], semtok=t_gb)
                nt = ntok // P
                S.dma('sp', xs[0][:], src[0:P, :], writes=[t_xs[0]], semtok=t_xs[0])
                for i in range(nt):
                    sl = i % 2
                    if i + 1 < nt:
                        S.dma('sp', xs[1 - sl][:], src[(i + 1) * P:(i + 2) * P, :], writes=[t_xs[1 - sl]],
                              semtok=t_xs[1 - sl])
                    S.op('act', lambda e: e.activation(out=junk[:], in_=xs[sl][:], func=AF.Square,
                                                       accum_out=ss[:, 0:1]),
                         reads=[t_xs[sl]], writes=[t_junk, t_ss])
                    S.op('act', lambda e: e.activation(out=ss[:, 1:2], in_=ss[:, 0:1], func=AF.Sqrt,
                                                       scale=1.0 / D, bias=epsT[:, 0:1]),
                         reads=[t_ss, t_c], writes=[t_ss])
                    S.op('dve', lambda e: e.reciprocal(out=ss[:, 2:3], in_=ss[:, 1:2]), reads=[t_ss], writes=[t_ss])
                    S.op('dve', lambda e: e.scalar_tensor_tensor(out=xb[:], in0=xs[sl][:], scalar=ss[:, 2:3],
                                                                 in1=gb[:], op0=ALU.mult, op1=ALU.mult),
                         reads=[t_xs[sl], t_ss, t_gb], writes=[t_xb])
                    for c8 in range(0, KC, 8):
                        nb = min(8, KC - c8)
                        bk = (c8 // 8) % 2

                        def tr(e, c8=c8, nb=nb, bk=bk):
                            for k in range(nb):
                                ins = e.transpose(out=PSB[bk][:, k * P:(k + 1) * P],
                                                  in_=xb[:, (c8 + k) * P:(c8 + k + 1) * P], identity=identb)
                            return ins
                        S.op('pe', tr, reads=[t_xb, t_c], writes=[tPS[bk]])
                        eng = 'act' if (c8 // 8) % 2 == 0 else 'dve'
                        dst = actT[:, c8:c8 + nb, col0 + i * P:col0 + (i + 1) * P]
                        srcp = PSB[bk][:, :nb * P].rearrange("p (c n) -> p c n", n=P)
                        if eng == 'act':
                            S.op('act', lambda e, dst=dst, srcp=srcp: e.copy(out=dst, in_=srcp),
                                 reads=[tPS[bk]], writes=[t_act])
                        else:
                            S.op('dve', lambda e, dst=dst, srcp=srcp: e.tensor_copy(out=dst, in_=srcp),
                                 reads=[tPS[bk]], writes=[t_act])
                S.end()

        def proj(w_ap, Kc, actT, t_act, ntok, tcol0, blocks, name):
            with ExitStack() as st:
                S.begin()
                lsb = lambda nm, shape, dt=F32: st.enter_context(nc.sbuf_tensor(un(nm), list(shape), dt))
                KH = min(Kc, 16)
                nh = Kc // KH
                tg = min(512, ntok)
                ng = ntok // tg
                assert ng <= 4
                NW = 3
                wst = [lsb("pj_wst%d" % i, [P, KH, P]) for i in range(NW)]
                wbf = [lsb("pj_wbf%d" % i, [P, KH, P], BF16) for i in range(NW)]
                t_wst, t_wbf = S.toks("wst", NW), S.toks("wbf", NW)
                NO = 4
                ob32 = [lsb("pj_o%d" % i, [P, 512]) for i in range(NO)]
                t_ob = S.toks("pjo", NO)
                tiles = [(b, h) for b in range(len(blocks)) for h in range(nh)]

                def load(ti):
                    b, h = tiles[ti]
                    c0, ncol = blocks[b][0], blocks[b][1]
                    sl = ti % NW
                    S.dma('sp', wst[sl][:, :, :ncol], rows_view(w_ap, h * KH * P, KH, c0, ncol),
                          writes=[t_wst[sl]], semtok=t_wst[sl])
                for ti in range(min(NW - 1, len(tiles))):
                    load(ti)
                oc = 0
                for ti, (b, h) in enumerate(tiles):
                    if ti + NW - 1 < len(tiles):
                        load(ti + NW - 1)
                    c0, ncol, epi, dst, ddt, bias = blocks[b]
                    sl = ti % NW
                    ce = 'dve' if ti % 2 == 0 else 'pool'
                    S.op(ce, lambda e: e.tensor_copy(out=wbf[sl][:, :, :ncol], in_=wst[sl][:, :, :ncol]),
                         reads=[t_wst[sl]], writes=[t_wbf[sl]])
                    pb = (b % 2) * 4

                    def mm(e, h=h, sl=sl, ncol=ncol, pb=pb):
                        for g in range(ng):
                            for c in range(KH):
                                ins = e.matmul(PS[pb + g][:ncol, :tg], lhsT=wbf[sl][:, c, :ncol],
                                               rhs=actT[:, h * KH + c, tcol0 + g * tg:tcol0 + (g + 1) * tg],
                                               start=(h == 0 and c == 0), stop=(h == nh - 1 and c == KH - 1))
                        return ins
                    S.op('pe', mm, reads=[t_wbf[sl], t_act], writes=[tPS[pb + g] for g in range(ng)])
                    if h == nh - 1:
                        for g in range(ng):
                            o = oc % NO
                            oc += 1
                            if ddt == BF16:
                                ot = ob32[o][:].bitcast(BF16)[:ncol, :tg]
                            else:
                                ot = ob32[o][:ncol, :tg]
                            src = PS[pb + g][:ncol, :tg]
                            if epi == 'copy':
                                S.op('act', lambda e, ot=ot, src=src: e.copy(out=ot, in_=src),
                                     reads=[tPS[pb + g]], writes=[t_ob[o]])
                            elif epi == 'silu':
                                S.op('act', lambda e, ot=ot, src=src: e.activation(out=ot, in_=src, func=AF.Silu),
                                     reads=[tPS[pb + g]], writes=[t_ob[o]])
                            elif epi == 'sigb':
                                S.op('act', lambda e, ot=ot, src=src, bias=bias: e.activation(
                                    out=ot, in_=src, func=AF.Sigmoid, bias=bias),
                                     reads=[tPS[pb + g], t_c], writes=[t_ob[o]])
                            S.dma('pool', dst[:, g * tg:(g + 1) * tg], ot, reads=[t_ob[o]], semtok=t_ob[o])
                S.end()

        bg_col = sb("bg_col", [P, 3 * D // P])
        cw_col = sb("cw_col", [P, 3 * CW // P])
        gn_col = sb("gn_col", [P, 4])
        load_cols(bg_col[:, :], b_gate, 3 * D // P, "bg")
        load_cols(cw_col[:, :], conv_w, 3 * CW // P, "cw")
        load_cols(gn_col[:, :], gla_norm, 4, "gn")

        o = cfg.off

        def blocks_for(lo, hi, epi, dst, ddt=F32, bias_fn=None):
            res = []
            c = lo
            while c < hi:
                n = min(P, hi - c)
                r0 = c - lo
                res.append((c, n, epi, dst(r0, n), ddt, bias_fn(r0 // P) if bias_fn else None))
                c += n
            return res

        with ExitStack() as st1:
            actT = st1.enter_context(nc.sbuf_tensor("actT", [P, KC, max(N, NPRE, NMEM)], BF16))
            t_act = S.tok("actT")
            norm_T(x_pre, norm_mix, actT, t_act, NPRE)
            bl = []
            bl += blocks_for(o[1], o[2], 'copy', lambda r, n: s_k[r:r + n, 0:NPRE])
            bl += blocks_for(o[2], o[3], 'copy', lambda r, n: s_v[r:r + n, 0:NPRE], BF16)
            bl += blocks_for(o[4], o[5], 'copy', lambda r, n: s_alr[r:r + n, 0:NPRE])
            proj(w_in, KC, actT, t_act, NPRE, 0, bl, "pre")
            bl = []
            bl += blocks_for(o[6], o[7], 'copy', lambda r, n: s_cc[r:r + n, 0:HALO])
            bl += blocks_for(o[7], o[8], 'copy', lambda r, n: s_ch[r:r + n, 0:HALO])
            proj(w_in, KC, actT, t_act, HALO, NPRE - HALO, bl, "pre2")
            if cfg.stages >= 2:
                norm_T(mem, mem_norm, actT, t_act, NMEM)
                bl = blocks_for(0, 2 * XW, 'copy', lambda r, n: s_mkv[r:r + n, 0:NMEM], BF16)
                proj(w_mem_kv, KC, actT, t_act, NMEM, 0, bl, "mkv")
            norm_T(x_main, norm_mix, actT, t_act, N)
            bl = []
            bl += blocks_for(o[0], o[1], 'copy', lambda r, n: s_q[r:r + n, 0:N])
            bl += blocks_for(o[1], o[2], 'copy', lambda r, n: s_k[r:r + n, NPRE:NT])
            bl += blocks_for(o[2], o[3], 'copy', lambda r, n: s_v[r:r + n, NPRE:NT], BF16)
            bl += blocks_for(o[3], o[4], 'silu', lambda r, n: s_r[r:r + n, 0:N])
            bl += blocks_for(o[4], o[5], 'copy', lambda r, n: s_alr[r:r + n, NPRE:NT])
            bl += blocks_for(o[5], o[6], 'copy', lambda r, n: s_cb[r:r + n, 0:N])
            bl += blocks_for(o[6], o[7], 'copy', lambda r, n: s_cc[r:r + n, HALO:HALO + N])
            bl += blocks_for(o[7], o[8], 'copy', lambda r, n: s_ch[r:r + n, HALO:HALO + N])
            bl += blocks_for(o[8], o[9], 'copy', lambda r, n: s_qx[r:r + n, 0:N], BF16)
            bl += blocks_for(o[9], o[10], 'sigb', lambda r, n: s_g[r:r + n, 0:N], F32,
                             lambda j: bg_col[:, j:j + 1])
            proj(w_in, KC, actT, t_act, N, 0, bl, "main")


        if cfg.stages >= 2:
            with ExitStack() as st:
                S.begin()
                lsb = lambda nm, shape, dt=F32: st.enter_context(nc.sbuf_tensor(un(nm), list(shape), dt))
                CWC = CW // P
                cc = [lsb("cv_cc", [P, N + 2]) for _ in range(2)]
                ch = [lsb("cv_ch", [P, N + 2]) for _ in range(2)]
                cbt = [lsb("cv_cb", [P, N]) for _ in range(2)]
                u = lsb("cv_u", [P, N + 2])
                y = lsb("cv_y", [P, N])
                z = [lsb("cv_z", [P, N], BF16) for _ in range(2)]
                t_ld = S.toks("cvld", 2)
                t_u, t_y = S.tok("cvu"), S.tok("cvy")
                t_z = S.toks("cvz", 2)

                def ld(fc):
                    sl = fc % 2
                    r = slice(fc * P, (fc + 1) * P)
                    S.dma('sp', cc[sl][:], s_cc[r, HALO - 2:HALO + N], writes=[t_ld[sl]], semtok=t_ld[sl])
                    S.dma('sp', ch[sl][:], s_ch[r, HALO - 2:HALO + N], writes=[t_ld[sl]], semtok=t_ld[sl])
                    S.dma('sp', cbt[sl][:], s_cb[r, 0:N], writes=[t_ld[sl]], semtok=t_ld[sl])
                ld(0)
                for fc in range(CWC):
                    sl = fc % 2
                    if fc + 1 < CWC:
                        ld(fc + 1)
                    S.op('dve', lambda e: e.tensor_tensor(out=u[:], in0=cc[sl][:], in1=ch[sl][:], op=ALU.mult),
                         reads=[t_ld[sl]], writes=[t_u])
                    S.op('dve', lambda e: e.tensor_scalar(out=y[:], in0=u[:, 2:N + 2],
                                                          scalar1=cw_col[:, 0 * CWC + fc:0 * CWC + fc + 1],
                                                          scalar2=None, op0=ALU.mult),
                         reads=[t_u, t_c], writes=[t_y])
                    S.op('dve', lambda e: e.scalar_tensor_tensor(out=y[:], in0=u[:, 1:N + 1],
                                                                 scalar=cw_col[:, 1 * CWC + fc:1 * CWC + fc + 1],
                                                                 in1=y[:], op0=ALU.mult, op1=ALU.add),
                         reads=[t_u, t_c], writes=[t_y])
                    S.op('dve', lambda e: e.scalar_tensor_tensor(out=y[:], in0=u[:, 0:N],
                                                                 scalar=cw_col[:, 2 * CWC + fc:2 * CWC + fc + 1],
                                                                 in1=y[:], op0=ALU.mult, op1=ALU.add),
                         reads=[t_u, t_c], writes=[t_y])
                    S.op('pool', lambda e: e.tensor_tensor(out=z[sl][:], in0=cbt[sl][:], in1=y[:], op=ALU.mult),
                         reads=[t_ld[sl], t_y], writes=[t_z[sl]])
                    S.dma('pool', s_zc[fc * P:(fc + 1) * P, 0:N], z[sl][:], reads=[t_z[sl]], semtok=t_z[sl])
                S.end()

        if cfg.stages >= 2:
            with ExitStack() as st:
                S.begin()
                lsb = lambda nm, shape, dt=F32: st.enter_context(nc.sbuf_tensor(un(nm), list(shape), dt))
                MC = NMEM // P
                XC = XW // P
                assert MC <= 2
                KT = lsb("xa_kt", [P, XC, NMEM], BF16)
                VT = lsb("xa_vt", [P, XC, NMEM], BF16)
                Vm = lsb("xa_vm", [P, MC, XW], BF16)
                qx = [lsb("xa_qx", [P, 4, TG], BF16) for _ in range(2)]
                pT = lsb("xa_pT", [P, MC, TG], BF16)
                rz = lsb("xa_rz", [P, TG])
                ox = [lsb("xa_ox", [P, 4, TG], BF16) for _ in range(2)]
                t_kv, t_vm, t_pT, t_rz = S.tok("xkv"), S.tok("xvm"), S.tok("xpT"), S.tok("xrz")
                t_qx, t_ox = S.toks("xqx", 2), S.toks("xox", 2)
                S.dma('sp', KT[:], rows_view(s_mkv, 0, XC, 0, NMEM), writes=[t_kv], semtok=t_kv)
                S.dma('sp', VT[:], rows_view(s_mkv, XW, XC, 0, NMEM), writes=[t_kv], semtok=t_kv)
                k = 0
                for mc in range(MC):
                    for f0 in range(0, XC, 8):
                        nb = min(8, XC - f0)
                        bk = k % 2
                        k += 1

                        def tr(e, mc=mc, f0=f0, nb=nb, bk=bk):
                            for i in range(nb):
                                ins = e.transpose(out=PSB[bk][:, i * P:(i + 1) * P],
                                                  in_=VT[:, f0 + i, mc * P:(mc + 1) * P], identity=identb)
                            return ins
                        S.op('pe', tr, reads=[t_kv, t_c], writes=[tPS[bk]])
                        S.op('dve', lambda e: e.tensor_copy(out=Vm[:, mc, f0 * P:(f0 + nb) * P],
                                                            in_=PSB[bk][:, :nb * P]),
                             reads=[tPS[bk]], writes=[t_vm])
                jobs = [(h, g) for h in range(XH) for g in range(NG)]

                def ldq(ji):
                    h, g = jobs[ji]
                    S.dma('sp', qx[ji % 2][:], rows_view(s_qx, h * 512, 4, g * TG, TG),
                          writes=[t_qx[ji % 2]], semtok=t_qx[ji % 2])
                ldq(0)
                for ji, (h, g) in enumerate(jobs):
                    sl = ji % 2
                    if ji + 1 < len(jobs):
                        ldq(ji + 1)

                    def sc(e):
                        for mc in range(MC):
                            for dc in range(4):
                                ins = e.matmul(PS[mc][:, :TG], lhsT=KT[:, h * 4 + dc, mc * P:(mc + 1) * P],
                                               rhs=qx[sl][:, dc, :], start=(dc == 0), stop=(dc == 3))
                        return ins
                    S.op('pe', sc, reads=[t_kv, t_qx[sl]], writes=[tPS[m] for m in range(MC)])

                    def ex(e):
                        for mc in range(MC):
                            ins = e.activation(out=pT[:, mc, :], in_=PS[mc][:, :TG], func=AF.Exp,
                                               scale=float(cfg.DH) ** -0.5)
                        return ins
                    S.op('act', ex, reads=[tPS[m] for m in range(MC)], writes=[t_pT])

                    def zz(e):
                        for mc in range(MC):
                            ins = e.matmul(PS[2][:, :TG], lhsT=onesb, rhs=pT[:, mc, :],
                                           start=(mc == 0), stop=(mc == MC - 1))
                        return ins
                    S.op('pe', zz, reads=[t_pT, t_c], writes=[tPS[2]])
                    S.op('dve', lambda e: e.reciprocal(out=rz[:], in_=PS[2][:, :TG]), reads=[tPS[2]], writes=[t_rz])

                    def ov(e):
                        for dc in range(4):
                            for mc in range(MC):
                                ins = e.matmul(PS[3 + dc][:, :TG],
                                               lhsT=Vm[:, mc, h * 512 + dc * P:h * 512 + (dc + 1) * P],
                                               rhs=pT[:, mc, :], start=(mc == 0), stop=(mc == MC - 1))
                        return ins
                    S.op('pe', ov, reads=[t_pT, t_vm], writes=[tPS[3 + d] for d in range(4)])

                    def nrm(e):
                        for dc in range(4):
                            ins = e.tensor_tensor(out=ox[sl][:, dc, :], in0=PS[3 + dc][:, :TG], in1=rz[:],
                                                  op=ALU.mult)
                        return ins
                    S.op('dve', nrm, reads=[tPS[3 + d] for d in range(4)] + [t_rz], writes=[t_ox[sl]])
                    S.dma('pool', rows_view(s_ox, h * 512, 4, g * TG, TG), ox[sl][:], reads=[t_ox[sl]],
                          semtok=t_ox[sl])
                S.end()


        if cfg.stages >= 3:
            with ExitStack() as st:
                S.begin()
                lsb = lambda nm, shape, dt=F32: st.enter_context(nc.sbuf_tensor(un(nm), list(shape), dt))
                QC, VC = QK // P, GV // P
                NQB = (QK + 511) // 512
                NVB = (GV + 1023) // 1024
                assert NQB <= 2 and NVB <= 2 and GH <= 4
                qw = min(512, QK)
                waug = lsb("g_waug", [17, QK])
                alr = [lsb("g_alr", [17, P]) for _ in range(2)]
                kF = [lsb("g_kF", [P, QC, P]) for _ in range(2)]
                vF = [lsb("g_vF", [P, VC, P], BF16) for _ in range(2)]
                qF = [lsb("g_qF", [P, QC, P]) for _ in range(2)]
                rF = [lsb("g_rF", [P, VC, P]) for _ in range(2)]
                en = lsb("g_en", [P, QK])
                ll = lsb("g_ll", [P, QK])
                ecs = lsb("g_ecs", [P, QK])
                ebs = lsb("g_ebs", [P, QC, P])
                enbs = lsb("g_enbs", [P, QC, P])
                koutb = lsb("g_kout", [P, QK], BF16)
                qtb = lsb("g_qt", [P, QC, P], BF16)
                kinb = lsb("g_kin", [P, QC, P], BF16)
                vtok = lsb("g_vtok", [P, GV], BF16)
                attb = lsb("g_att", [P, GH, P], BF16)
                Sst = lsb("g_S", [P, QC, 512])
                Sbf = lsb("g_Sbf", [P, QC, 512], BF16)
                junk = lsb("g_junk", [P, 512], BF16)
                ssg = lsb("g_ss", [P, 3 * GH])
                onb = lsb("g_on", [P, GV], BF16)
                ogr = [lsb("g_ogr", [P, VC, P], BF16) for _ in range(2)]
                t_waug = S.tok("gwaug")
                t_ld = S.toks("gld", 2)
                t_en, t_ll, t_ecs, t_ebs, t_enbs = S.tok("gen"), S.tok("gll"), S.tok("gecs"), S.tok("gebs"), S.tok("genbs")
                t_kout, t_qt, t_kin, t_vtok, t_att = S.tok("gkout"), S.tok("gqt"), S.tok("gkin"), S.tok("gvtok"), S.tok("gatt")
                t_S, t_Sbf = S.toks("gS", QC), S.toks("gSbf", QC)
                t_junk, t_ss, t_on = S.tok("gjunk"), S.tok("gss"), S.tok("gon")
                t_ogr = S.toks("gogr", 2)
                S.dma('sp', waug[:], w_aug[:, :], writes=[t_waug], semtok=t_waug)
                for i in range(2):
                    S.op('dve', lambda e: e.memset(alr[i][:], 1.0), writes=[t_ld[i]])
                S.op('dve', lambda e: e.memset(Sst[:], 0.0), writes=t_S)
                S.op('pool', lambda e: e.memset(Sbf[:], 0.0), writes=t_Sbf)
                nchunks = NT // P

                def ld(ci):
                    sl = ci % 2
                    t0 = ci * P
                    S.dma('sp', alr[sl][0:16, :], s_alr[0:16, t0:t0 + P], writes=[t_ld[sl]], semtok=t_ld[sl])
                    S.dma('sp', kF[sl][:], rows_view(s_k, 0, QC, t0, P), writes=[t_ld[sl]], semtok=t_ld[sl])
                    S.dma('sp', vF[sl][:], rows_view(s_v, 0, VC, t0, P), writes=[t_ld[sl]], semtok=t_ld[sl])
                    if t0 >= NPRE:
                        tm = t0 - NPRE
                        S.dma('sp', qF[sl][:], rows_view(s_q, 0, QC, tm, P), writes=[t_ld[sl]], semtok=t_ld[sl])
                        S.dma('sp', rF[sl][:], rows_view(s_r, 0, VC, tm, P), writes=[t_ld[sl]], semtok=t_ld[sl])
                ld(0)
                for ci in range(nchunks):
                    sl = ci % 2
                    t0 = ci * P
                    main = t0 >= NPRE
                    tm = t0 - NPRE
                    if ci + 1 < nchunks:
                        ld(ci + 1)
                    def f_z(e):
                        for j in range(NQB):
                            ins = e.matmul(PS[j][:, :qw], lhsT=alr[sl][0:17, :], rhs=waug[0:17, j * qw:(j + 1) * qw],
                                           start=True, stop=True)
                        return ins
                    S.op('pe', f_z, reads=[t_ld[sl], t_waug], writes=tPS[0:NQB])
                    def f_en(e):
                        for j in range(NQB):
                            ins = e.activation(out=en[:, j * qw:(j + 1) * qw], in_=PS[j][:, :qw], func=AF.Exp, scale=-1.0)
                        return ins
                    S.op('act', f_en, reads=tPS[0:NQB], writes=[t_en])
                    S.op('act', lambda e: e.activation(out=ll[:], in_=en[:], func=AF.Ln, bias=oneT[:, 0:1]),
                         reads=[t_en, t_c], writes=[t_ll])
                    def f_cum(e):
                        for j in range(NQB):
                            ins = e.matmul(PS[2 + j][:, :qw], lhsT=triL, rhs=ll[:, j * qw:(j + 1) * qw],
                                           start=True, stop=True)
                        for dc in range(QC):
                            ins = e.matmul(PS[4 + dc // 4][:, (dc % 4) * P:(dc % 4 + 1) * P],
                                           lhsT=ll[:, dc * P:(dc + 1) * P], rhs=triU, start=True, stop=True)
                        for dc in range(QC):
                            ins = e.transpose(out=PS[6 + dc // 4][:, (dc % 4) * P:(dc % 4 + 1) * P],
                                              in_=kF[sl][:, dc, :], identity=identf)
                        return ins
                    S.op('pe', f_cum, reads=[t_ll, t_c, t_ld[sl]],
                         writes=tPS[2:2 + NQB] + tPS[4:4 + NQB] + tPS[6:6 + NQB])
                    def f_ec(e):
                        for j in range(NQB):
                            ins = e.activation(out=ecs[:, j * qw:(j + 1) * qw], in_=PS[2 + j][:, :qw], func=AF.Exp)
                        return ins
                    S.op('act', f_ec, reads=tPS[2:2 + NQB], writes=[t_ecs])
                    def f_kout(e):
                        for j in range(NQB):
                            ins = e.tensor_tensor(out=koutb[:, j * qw:(j + 1) * qw], in0=PS[6 + j][:, :qw],
                                                  in1=ecs[:, j * qw:(j + 1) * qw], op=ALU.mult)
                        return ins
                    S.op('dve', f_kout, reads=tPS[6:6 + NQB] + [t_ecs], writes=[t_kout])
                    def f_eb(e, dst=ebs, sc=1.0):
                        for j in range(NQB):
                            n4 = min(4, QC - 4 * j)
                            ins = e.activation(out=dst[:, 4 * j:4 * j + n4, :],
                                               in_=PS[4 + j][:, :n4 * P].rearrange("p (c n) -> p c n", n=P),
                                               func=AF.Exp, scale=sc)
                        return ins
                    S.op('act', f_eb, reads=tPS[4:4 + NQB], writes=[t_ebs])
                    if main:
                        S.op('act', lambda e: f_eb(e, enbs, -1.0), reads=tPS[4:4 + NQB], writes=[t_enbs])
                        S.op('dve', lambda e: e.scalar_tensor_tensor(out=qtb[:], in0=qF[sl][:], scalar=float(cfg.DK) ** -0.5,
                                                                     in1=ebs[:], op0=ALU.mult, op1=ALU.mult),
                             reads=[t_ld[sl], t_ebs], writes=[t_qt])
                        S.op('dve', lambda e: e.tensor_tensor(out=kinb[:], in0=kF[sl][:], in1=enbs[:], op=ALU.mult),
                             reads=[t_ld[sl], t_enbs], writes=[t_kin])
                    def f_vt(e):
                        for vc in range(VC):
                            ins = e.transpose(out=PSB[vc // 8][:, (vc % 8) * P:(vc % 8 + 1) * P],
                                              in_=vF[sl][:, vc, :], identity=identb)
                        return ins
                    S.op('pe', f_vt, reads=[t_ld[sl], t_c], writes=tPS[0:NVB])
                    def f_vc(e):
                        for b in range(NVB):
                            w = min(1024, GV - b * 1024)
                            ins = e.copy(out=vtok[:, b * 1024:b * 1024 + w], in_=PSB[b][:, :w])
                        return ins
                    S.op('act', f_vc, reads=tPS[0:NVB], writes=[t_vtok])
                    if main:
                        def f_att(e):
                            for h in range(GH):
                                for dd in range(2):
                                    ins = e.matmul(PS[2][:, h * P:(h + 1) * P], lhsT=kinb[:, 2 * h + dd, :],
                                                   rhs=qtb[:, 2 * h + dd, :], start=(dd == 0), stop=(dd == 1))
                            return ins
                        S.op('pe', f_att, reads=[t_kin, t_qt], writes=[tPS[2]])
                        def f_mask(e):
                            for h in range(GH):
                                ins = e.tensor_tensor(out=attb[:, h, :], in0=PS[2][:, h * P:(h + 1) * P], in1=mask01f,
                                                      op=ALU.mult)
                            return ins
                        S.op('dve', f_mask, reads=[tPS[2], t_c], writes=[t_att])
                        def f_o(e):
                            for h in range(GH):
                                e.matmul(PS[4 + h][:, :512], lhsT=attb[:, h, :], rhs=vtok[:, h * 512:(h + 1) * 512],
                                         start=True, stop=False)
                                for dd in range(2):
                                    ins = e.matmul(PS[4 + h][:, :512], lhsT=qtb[:, 2 * h + dd, :],
                                                   rhs=Sbf[:, 2 * h + dd, :], start=False, stop=(dd == 1))
                            return ins
                        S.op('pe', f_o, reads=[t_att, t_vtok, t_qt] + t_Sbf, writes=tPS[4:4 + GH])
                        def f_sq(e):
                            for h in range(GH):
                                ins = e.activation(out=junk[:], in_=PS[4 + h][:, :512], func=AF.Square,
                                                   accum_out=ssg[:, h:h + 1])
                            return ins
                        S.op('act', f_sq, reads=tPS[4:4 + GH], writes=[t_junk, t_ss])
                        S.op('act', lambda e: e.activation(out=ssg[:, GH:2 * GH], in_=ssg[:, 0:GH], func=AF.Sqrt,
                                                           scale=1.0 / 512.0, bias=epsT[:, 0:1]),
                             reads=[t_ss, t_c], writes=[t_ss])
                        S.op('dve', lambda e: e.reciprocal(out=ssg[:, 2 * GH:3 * GH], in_=ssg[:, GH:2 * GH]),
                             reads=[t_ss], writes=[t_ss])
                        def f_on(e):
                            for h in range(GH):
                                ins = e.activation(out=onb[:, h * 512:(h + 1) * 512], in_=PS[4 + h][:, :512],
                                                   func=AF.Copy, scale=ssg[:, 2 * GH + h:2 * GH + h + 1])
                            return ins
                        S.op('act', f_on, reads=tPS[4:4 + GH] + [t_ss], writes=[t_on])
                        def f_ot(e):
                            for vc in range(VC):
                                ins = e.transpose(out=PSB[vc // 8][:, (vc % 8) * P:(vc % 8 + 1) * P],
                                                  in_=onb[:, vc * P:(vc + 1) * P], identity=identb)
                            return ins
                        S.op('pe', f_ot, reads=[t_on, t_c], writes=tPS[0:NVB])
                        osl = (ci // 1) % 2
                        def f_ogr(e):
                            for vc in range(VC):
                                ins = e.scalar_tensor_tensor(out=ogr[osl][:, vc, :],
                                                             in0=PSB[vc // 8][:, (vc % 8) * P:(vc % 8 + 1) * P],
                                                             scalar=gn_col[:, vc % 4:vc % 4 + 1], in1=rF[sl][:, vc, :],
                                                             op0=ALU.mult, op1=ALU.mult)
                            return ins
                        S.op('dve', f_ogr, reads=tPS[0:NVB] + [t_c, t_ld[sl]], writes=[t_ogr[osl]])
                        S.dma('pool', rows_view(s_ogr, 0, VC, tm, P), ogr[osl][:], reads=[t_ogr[osl]], semtok=t_ogr[osl])
                    for idx in range(QC):
                        h = idx // 2
                        bk = [2, 3, 0, 1][idx % 4]
                        S.op('pe', lambda e: e.matmul(PS[bk][:, :512], lhsT=koutb[:, idx * P:(idx + 1) * P],
                                                      rhs=vtok[:, h * 512:(h + 1) * 512], start=True, stop=True),
                             reads=[t_kout, t_vtok], writes=[tPS[bk]])
                        S.op('dve', lambda e: e.scalar_tensor_tensor(out=Sst[:, idx, :], in0=Sst[:, idx, :],
                                                                     scalar=ebs[:, idx, P - 1:P], in1=PS[bk][:, :512],
                                                                     op0=ALU.mult, op1=ALU.add),
                             reads=[tPS[bk], t_ebs], writes=[t_S[idx]])
                        S.op('pool', lambda e: e.tensor_copy(out=Sbf[:, idx, :], in_=Sst[:, idx, :]),
                             reads=[t_S[idx]], writes=[t_Sbf[idx]])
                S.end()


        s_mg = dscr("s_mg", [D, N], BF16)
        TGm = min(1024, N)
        NGm = N // TGm
        NH4 = (TGm + 511) // 512
        hw4 = min(512, TGm)
        if cfg.stages >= 4:
            with ExitStack() as st:
                S.begin()
                lsb = lambda nm, shape, dt=F32: st.enter_context(nc.sbuf_tensor(un(nm), list(shape), dt))
                brs = [(w_br_gla, s_ogr, GV // P), (w_br_conv, s_zc, CW // P), (w_br_xa, s_ox, XW // P)]
                MXC = max(b[2] for b in brs)
                JW = 2
                bin_ = [lsb("m_in", [P, b[2], TGm], BF16) for b in brs]
                NW = 2
                wst = [lsb("m_wst", [P, MXC, JW * P]) for _ in range(NW)]
                wbf = [lsb("m_wbf", [P, MXC, JW * P], BF16) for _ in range(NW)]
                gt = [lsb("m_gt", [P, 3, TGm]) for _ in range(2)]
                mt = [[lsb("m_mt", [P, TGm]) for _ in range(3)] for _ in range(JW)]
                mo = [lsb("m_mo", [P, TGm], BF16) for _ in range(2)]
                t_in = S.tok("min")
                t_wst, t_wbf = S.toks("mwst", NW), S.toks("mwbf", NW)
                t_gt, t_mo = S.toks("mgt", 2), S.toks("mmo", 2)
                t_mt = [S.toks("mmt%d" % i, 3) for i in range(JW)]
                pq = 0
                for g in range(NGm):
                    for bi, (wb, sc_, nck) in enumerate(brs):
                        S.dma('sp', bin_[bi][:], rows_view(sc_, 0, nck, g * TGm, TGm), writes=[t_in], semtok=t_in)
                    tiles = [(jp, bi) for jp in range(KC // JW) for bi in range(3)]

                    def ldw(ti):
                        jp, bi = tiles[ti]
                        wb, sc_, nck = brs[bi]
                        sl = ti % NW
                        S.dma('sp', wst[sl][:, :nck, :], rows_view(wb, 0, nck, jp * JW * P, JW * P), writes=[t_wst[sl]],
                              semtok=t_wst[sl])
                    for ti in range(min(NW - 1, len(tiles))):
                        ldw(ti)
                    for ti, (jp, bi) in enumerate(tiles):
                        if ti + NW - 1 < len(tiles):
                            ldw(ti + NW - 1)
                        wb, sc_, nck = brs[bi]
                        sl = ti % NW
                        if bi == 0:
                            for jj in range(JW):
                                j = jp * JW + jj
                                for b3 in range(3):
                                    S.dma('sp', gt[j % 2][:, b3, :],
                                          s_g[b3 * D + j * P:b3 * D + (j + 1) * P, g * TGm:(g + 1) * TGm],
                                          writes=[t_gt[j % 2]], semtok=t_gt[j % 2])
                        S.op('pool' if ti % 2 else 'dve',
                             lambda e: e.tensor_copy(out=wbf[sl][:, :nck, :], in_=wst[sl][:, :nck, :]),
                             reads=[t_wst[sl]], writes=[t_wbf[sl]])
                        for jj in range(JW):
                            j = jp * JW + jj
                            gsl = j % 2
                            pb = (pq % (8 // NH4)) * NH4
                            pq += 1

                            def mm(e):
                                for hf in range(NH4):
                                    for c in range(nck):
                                        ins = e.matmul(PS[pb + hf][:, :hw4], lhsT=wbf[sl][:, c, jj * P:(jj + 1) * P],
                                                       rhs=bin_[bi][:, c, hf * hw4:(hf + 1) * hw4],
                                                       start=(c == 0), stop=(c == nck - 1))
                                return ins
                            S.op('pe', mm, reads=[t_wbf[sl], t_in], writes=tPS[pb:pb + NH4])

                            def gm(e):
                                for hf in range(NH4):
                                    ins = e.tensor_tensor(out=mt[jj][bi][:, hf * hw4:(hf + 1) * hw4],
                                                          in0=PS[pb + hf][:, :hw4],
                                                          in1=gt[gsl][:, bi, hf * hw4:(hf + 1) * hw4], op=ALU.mult)
                                return ins
                            S.op('dve', gm, reads=tPS[pb:pb + NH4] + [t_gt[gsl]], writes=[t_mt[jj][bi]])
                            if bi == 2:
                                S.op('pool', lambda e: e.tensor_tensor(out=mt[jj][0][:], in0=mt[jj][0][:], in1=mt[jj][1][:],
                                                                       op=ALU.add),
                                     reads=[t_mt[jj][1]], writes=[t_mt[jj][0]])
                                S.op('pool', lambda e: e.tensor_tensor(out=mo[gsl][:], in0=mt[jj][0][:], in1=mt[jj][2][:],
                                                                       op=ALU.add),
                                     reads=[t_mt[jj][0], t_mt[jj][2]], writes=[t_mo[gsl]])
                                S.dma('pool', s_mg[j * P:(j + 1) * P, g * TGm:(g + 1) * TGm], mo[gsl][:],
                                      reads=[t_mo[gsl]], semtok=t_mo[gsl])
                S.end()

        if cfg.stages >= 4:
            with ExitStack() as st:
                S.begin()
                lsb = lambda nm, shape, dt=F32: st.enter_context(nc.sbuf_tensor(un(nm), list(shape), dt))
                cgw = min(512, D)
                qw4 = min(256, cgw)
                nq = cgw // qw4
                KHo = min(KC, 16)
                nkh = KC // KHo
                mg = lsb("o_mg", [P, KC, TGm], BF16)
                wst = [lsb("o_wst", [P, KHo, qw4]) for _ in range(2)]
                wbf = [lsb("o_wbf", [P, KC, cgw], BF16) for _ in range(2)]
                xt = [lsb("o_xt", [P, cgw]) for _ in range(2)]
                ot = [lsb("o_ot", [P, cgw]) for _ in range(2)]
                t_mg = S.tok("omg")
                t_wst, t_wbf, t_xt, t_ot = S.toks("owst", 2), S.toks("owbf", 2), S.toks("oxt", 2), S.toks("oot", 2)
                k = 0
                wi = 0
                for g in range(NGm):
                    S.dma('sp', mg[:], rows_view(s_mg, 0, KC, g * TGm, TGm), writes=[t_mg], semtok=t_mg)
                    for cg in range(D // cgw):
                        bsl = (g * (D // cgw) + cg) % 2
                        for q in range(nq):
                            for kh in range(nkh):
                                sl = wi % 2
                                wi += 1
                                S.dma('sp', wst[sl][:], rows_view(w_o, kh * KHo * P, KHo, cg * cgw + q * qw4, qw4),
                                      writes=[t_wst[sl]], semtok=t_wst[sl])
                                S.op('pool' if wi % 2 else 'dve',
                                     lambda e: e.tensor_copy(out=wbf[bsl][:, kh * KHo:(kh + 1) * KHo, q * qw4:(q + 1) * qw4],
                                                             in_=wst[sl][:]),
                                     reads=[t_wst[sl]], writes=[t_wbf[bsl]])
                        for tt in range(TGm // P):
                            sl = k % 2
                            pb = k % 4
                            k += 1
                            r0 = g * TGm + tt * P
                            S.dma('sp', xt[sl][:], x_main[r0:r0 + P, cg * cgw:(cg + 1) * cgw], writes=[t_xt[sl]],
                                  semtok=t_xt[sl])

                            def mm(e):
                                for c in range(KC):
                                    ins = e.matmul(PS[pb][:, :cgw], lhsT=mg[:, c, tt * P:(tt + 1) * P], rhs=wbf[bsl][:, c, :],
                                                   start=(c == 0), stop=(c == KC - 1))
                                return ins
                            S.op('pe', mm, reads=[t_mg, t_wbf[bsl]], writes=[tPS[pb]])
                            S.op('dve', lambda e: e.tensor_tensor(out=ot[sl][:], in0=PS[pb][:, :cgw], in1=xt[sl][:],
                                                                  op=ALU.add),
                                 reads=[tPS[pb], t_xt[sl]], writes=[t_ot[sl]])
                            S.dma('pool', s_x2[r0:r0 + P, cg * cgw:(cg + 1) * cgw], ot[sl][:], reads=[t_ot[sl]],
                                  semtok=t_ot[sl])
                S.end()

        NTT = N // P
        HC = 2 * PH
        s_hn2 = dscr("s_hn2", [D, N], BF16)
        s_gg = dscr("s_gg", [NE, N], BF16)
        if cfg.stages >= 5:
            with ExitStack() as st:
                actT2 = st.enter_context(nc.sbuf_tensor(un("actT2"), [P, KC, N], BF16))
                t_act2 = S.tok("actT2")
                norm_T(s_x2, norm_ffn, actT2, t_act2, N)
                S.dma('sp', rows_view(s_hn2, 0, KC, 0, N), actT2[:], reads=[t_act2], semtok=t_act2)
                bl = blocks_for(0, PQ, 'copy', lambda r, n: s_qp[r:r + n, 0:N], BF16)
                proj(peer_wq, KC, actT2, t_act2, N, 0, bl, "pwq")
                S.barrier()

        if cfg.stages >= 5:
            with ExitStack() as st:
                S.begin()
                lsb = lambda nm, shape, dt=F32: st.enter_context(nc.sbuf_tensor(un(nm), list(shape), dt))
                NSB = (HC * P + 511) // 512
                assert NSB <= 4
                EB = 16
                NEB = P // EB
                NEC = NE // P
                HT = min(1024, N)
                NPASS = N // HT
                NGh = HT // TG
                skT = lsb("p_skT", [P, HC, P], BF16)
                sktmp = [lsb("p_sktmp", [P, P]) for _ in range(2)]
                qp = [lsb("p_qp", [P, HC, P], BF16) for _ in range(2)]
                ssb = lsb("p_ssb", [P, HC, P])
                swk = lsb("p_swk", [P, HC, P])
                sm = lsb("p_sm", [P, HC, P])
                sv = lsb("p_sv", [P, HC, 16])
                cand = lsb("p_cand", [P, PH, 256])
                c8 = lsb("p_c8", [P, PH, 24])
                sml = lsb("p_sml", [P, 6, PH])
                j16 = lsb("p_j16", [P, 16])
                NSL = 3
                Dt = [lsb("p_D", [P, EB, P]) for _ in range(NSL)]
                Et = [lsb("p_E", [P, EB, P], BF16) for _ in range(NSL)]
                Wh = [lsb("p_Wh", [P, EB * P], BF16) for _ in range(NSL)]
                wtT = [lsb("p_wtT", [P, EB, P], BF16) for _ in range(2)]
                t_skT = S.tok("pskT")
                t_sktmp, t_qp = S.toks("psktmp", 2), S.toks("pqp", 2)
                t_ssb, t_swk, t_sm, t_sv = S.tok("pssb"), S.tok("pswk"), S.tok("psm"), S.tok("psv")
                t_cand, t_c8, t_sml, t_j16 = S.tok("pcand"), S.tok("pc8"), S.tok("psml"), S.tok("pj16")
                t_D, t_E, t_Wh, t_wtT = S.toks("pD", NSL), S.toks("pE", NSL), S.toks("pWh", NSL), S.toks("pwtT", 2)
                HD = max(D // 4, P)
                NHf = D // HD
                KH2 = HD // P
                actT2h = lsb("u_act", [P, KC, HT], BF16)
                ust = [lsb("u_st", [P, HD]) for _ in range(2)]
                UT = [lsb("u_T", [P, KC, P], BF16) for _ in range(2)]
                Gs = [lsb("u_G", [P, HT], BF16) for _ in range(2)]
                t_acth = S.tok("uacth")
                t_ust, t_UT, t_G = S.toks("uust", 2), S.toks("uUT", 2), S.toks("uG", 2)
                for hc in range(HC):
                    sl = hc % 2
                    S.dma('sp', sktmp[sl][:], peer_sk[hc * P:(hc + 1) * P, :], writes=[t_sktmp[sl]], semtok=t_sktmp[sl])
                    S.op('pe', lambda e: e.transpose(out=PS[sl][:, :P], in_=sktmp[sl][:], identity=identf),
                         reads=[t_sktmp[sl], t_c], writes=[tPS[sl]])
                    S.op('dve', lambda e: e.tensor_copy(out=skT[:, hc, :], in_=PS[sl][:, :P]), reads=[tPS[sl]],
                         writes=[t_skT])

                def each(n, fn):
                    def g(e):
                        for i in range(n):
                            ins = fn(e, i)
                        return ins
                    return g

                def ldq(ti):
                    S.dma('sp', qp[ti % 2][:], rows_view(s_qp, 0, HC, ti * P, P), writes=[t_qp[ti % 2]],
                          semtok=t_qp[ti % 2])
                wkc = [0]

                def route_tile(ti, filler):
                    sl = ti % 2
                    if ti + 1 < NTT:
                        ldq(ti + 1)

                    def f_sc(e):
                        for hc in range(HC):
                            ins = e.matmul(PS[hc // 4][:, (hc % 4) * P:(hc % 4 + 1) * P], lhsT=qp[sl][:, hc, :],
                                           rhs=skT[:, hc, :], start=True, stop=True)
                        return ins
                    S.op('pe', f_sc, reads=[t_qp[sl], t_skT], writes=tPS[0:NSB])

                    def f_cp(e):
                        for b in range(NSB):
                            n4 = min(4, HC - 4 * b)
                            ins = e.copy(out=ssb[:, 4 * b:4 * b + n4, :],
                                         in_=PS[b][:, :n4 * P].rearrange("p (c n) -> p c n", n=P))
                        return ins
                    S.op('act', f_cp, reads=tPS[0:NSB], writes=[t_ssb])
                    S.op('dve', each(HC, lambda e, hc: e.max(out=sv[:, hc, 0:8], in_=ssb[:, hc, :])),
                         reads=[t_ssb], writes=[t_sv])
                    S.op('dve', each(HC, lambda e, hc: e.match_replace(out=swk[:, hc, :], in_to_replace=sv[:, hc, 0:8],
                                                                       in_values=ssb[:, hc, :], imm_value=-1.0e30)),
                         reads=[t_ssb, t_sv], writes=[t_swk])
                    S.op('dve', each(HC, lambda e, hc: e.max(out=sv[:, hc, 8:16], in_=swk[:, hc, :])),
                         reads=[t_swk], writes=[t_sv])
                    S.op('dve', each(HC, lambda e, hc: e.tensor_scalar(out=swk[:, hc, :], in0=ssb[:, hc, :],
                                                                       scalar1=sv[:, hc, 15:16], scalar2=1.0,
                                                                       op0=ALU.is_ge, op1=ALU.subtract)),
                         reads=[t_ssb, t_sv], writes=[t_swk])
                    S.op('dve', each(HC, lambda e, hc: e.scalar_tensor_tensor(out=sm[:, hc, :], in0=swk[:, hc, :],
                                                                              scalar=NEG_BIG, in1=ssb[:, hc, :],
                                                                              op0=ALU.mult, op1=ALU.add)),
                         reads=[t_swk, t_ssb], writes=[t_sm])
                    S.op('dve', each(PH, lambda e, h: e.tensor_tensor(
                        out=cand[:, h, :].rearrange("p (i j) -> p i j", j=16),
                        in0=sv[:, 2 * h, :].unsqueeze(2).to_broadcast([P, 16, 16]),
                        in1=sv[:, 2 * h + 1, :].unsqueeze(1).to_broadcast([P, 16, 16]), op=ALU.add)),
                         reads=[t_sv], writes=[t_cand])
                    for rnd in range(3):
                        S.op('dve', each(PH, lambda e, h: e.max(out=c8[:, h, 8 * rnd:8 * rnd + 8], in_=cand[:, h, :])),
                             reads=[t_cand], writes=[t_c8])
                        if rnd < 2:
                            S.op('dve', each(PH, lambda e, h: e.match_replace(
                                out=cand[:, h, :], in_to_replace=c8[:, h, 8 * rnd:8 * rnd + 8], in_values=cand[:, h, :],
                                imm_value=-1.0e30)), reads=[t_c8], writes=[t_cand])
                    S.op('dve', lambda e: e.tensor_tensor(out=sml[:, 0, :], in0=c8[:, :, 15], in1=c8[:, :, 16], op=ALU.add),
                         reads=[t_c8], writes=[t_sml])
                    S.op('dve', lambda e: e.tensor_scalar(out=sml[:, 0, :], in0=sml[:, 0, :], scalar1=0.5, scalar2=None,
                                                          op0=ALU.mult), reads=[t_sml], writes=[t_sml])
                    S.op('dve', lambda e: e.tensor_scalar(out=sml[:, 1, :], in0=c8[:, :, 0], scalar1=-1.0, scalar2=None,
                                                          op0=ALU.mult), reads=[t_c8], writes=[t_sml])
                    S.op('act', each(PH, lambda e, h: e.activation(out=j16[:], in_=c8[:, h, 0:16], func=AF.Exp,
                                                                   bias=sml[:, 1, h:h + 1],
                                                                   accum_out=sml[:, 2, h:h + 1])),
                         reads=[t_c8, t_sml], writes=[t_j16, t_sml])
                    S.op('act', lambda e: e.activation(out=sml[:, 3, :], in_=sml[:, 2, :], func=AF.Ln),
                         reads=[t_sml], writes=[t_sml])
                    S.op('dve', lambda e: e.tensor_tensor(out=sml[:, 4, :], in0=sml[:, 1, :], in1=sml[:, 3, :],
                                                          op=ALU.subtract),
                         reads=[t_sml], writes=[t_sml])
                    for eb in range(NEB):
                        wsl = eb % 2
                        for h in range(PH):
                            ds = wkc[0] % NSL
                            wkc[0] += 1
                            S.op('pool', lambda e: e.tensor_tensor(
                                out=Dt[ds][:],
                                in0=sm[:, 2 * h, eb * EB:(eb + 1) * EB].unsqueeze(2).to_broadcast([P, EB, P]),
                                in1=sm[:, 2 * h + 1, :].unsqueeze(1).to_broadcast([P, EB, P]), op=ALU.add),
                                 reads=[t_sm], writes=[t_D[ds]])
                            S.op('act', lambda e: e.activation(out=Et[ds][:], in_=Dt[ds][:], func=AF.Exp,
                                                               bias=sml[:, 4, h:h + 1]),
                                 reads=[t_D[ds], t_sml], writes=[t_E[ds]])
                            S.op('dve', lambda e: e.scalar_tensor_tensor(
                                out=Wh[ds][:], in0=Dt[ds][:].rearrange("p a b -> p (a b)"), scalar=sml[:, 0, h:h + 1],
                                in1=Et[ds][:].rearrange("p a b -> p (a b)"), op0=ALU.is_ge, op1=ALU.mult),
                                 reads=[t_D[ds], t_E[ds], t_sml], writes=[t_Wh[ds]])
                            filler()

                            def f_tr(e):
                                for c in range(EB):
                                    ins = e.matmul(PS[c // 4][:, (c % 4) * P:(c % 4 + 1) * P],
                                                   lhsT=Wh[ds][:, c * P:(c + 1) * P], rhs=identb,
                                                   start=(h == 0 and c % 4 == 0), stop=(h == PH - 1 and c % 4 == 3),
                                                   skip_group_check=True)
                                return ins
                            S.op('pe', f_tr, reads=[t_Wh[ds], t_c], writes=tPS[0:4])

                        def f_ev(e):
                            for b in range(4):
                                ins = e.copy(out=wtT[wsl][:, 4 * b:4 * b + 4, :],
                                             in_=PS[b][:, :].rearrange("p (c n) -> p c n", n=P))
                            return ins
                        S.op('act', f_ev, reads=tPS[0:4], writes=[t_wtT[wsl]])
                        S.dma('act', rows_view(s_wt, eb * EB * P, EB, ti * P, P), wtT[wsl][:], reads=[t_wtT[wsl]],
                              semtok=t_wtT[wsl])

                ujobs = [(p_, c, hf) for p_ in range(NPASS) for c in range(NEC) for hf in range(NHf)]
                ujn = [0]
                tbn = [0]

                def ldu(k):
                    p_, c, hf = ujobs[k]
                    S.dma('sp', ust[k % 2][:], peer_u[c * P:(c + 1) * P, hf * HD:(hf + 1) * HD], writes=[t_ust[k % 2]],
                          semtok=t_ust[k % 2])

                def act_chunk(p_, c):
                    usl = c % 2
                    for hf in range(NHf):
                        k = ujn[0]
                        ujn[0] += 1
                        assert ujobs[k] == (p_, c, hf)
                        sl = k % 2
                        if k + 1 < len(ujobs):
                            ldu(k + 1)
                        for k4 in range(0, KH2, 4):
                            n4 = min(4, KH2 - k4)
                            bk = 4 + tbn[0] % 2
                            tbn[0] += 1

                            def f_tr(e):
                                for i in range(n4):
                                    ins = e.transpose(out=PS[bk][:, i * P:(i + 1) * P],
                                                      in_=ust[sl][:, (k4 + i) * P:(k4 + i + 1) * P], identity=identf)
                                return ins
                            S.op('pe', f_tr, reads=[t_ust[sl], t_c], writes=[tPS[bk]])
                            dstu = UT[usl][:, hf * KH2 + k4:hf * KH2 + k4 + n4, :]
                            srcu = PS[bk][:, :n4 * P].rearrange("p (c n) -> p c n", n=P)
                            S.op('act', lambda e: e.copy(out=dstu, in_=srcu), reads=[tPS[bk]], writes=[t_UT[usl]])
                            yield

                    for g in range(NGh):
                        def f_a(e):
                            for dc in range(KC):
                                ins = e.matmul(PS[6 + g][:, :TG], lhsT=UT[usl][:, dc, :],
                                               rhs=actT2h[:, dc, g * TG:(g + 1) * TG], start=(dc == 0), stop=(dc == KC - 1))
                            return ins
                        S.op('pe', f_a, reads=[t_UT[usl], t_acth], writes=[tPS[6 + g]])
                        yield

                    def f_g(e):
                        for g in range(NGh):
                            ins = e.activation(out=Gs[usl][:, g * TG:(g + 1) * TG], in_=PS[6 + g][:, :TG], func=AF.Gelu)
                        return ins
                    S.op('act', f_g, reads=tPS[6:6 + NGh], writes=[t_G[usl]])
                    S.dma('act', s_gg[c * P:(c + 1) * P, p_ * HT:(p_ + 1) * HT], Gs[usl][:], reads=[t_G[usl]],
                          semtok=t_G[usl])
                    yield

                def act_stream(p_):
                    for c in range(NEC):
                        yield from act_chunk(p_, c)

                ldq(0)
                ldu(0)
                TPP = HT // P
                for p_ in range(NPASS):
                    S.dma('sp', actT2h[:], rows_view(s_hn2, 0, KC, p_ * HT, HT), writes=[t_acth], semtok=t_acth)
                    n_micro = NEC * (NHf * ((KH2 + 3) // 4) + NGh + 1)
                    n_units = TPP * NEB * PH
                    per = (n_micro + n_units - 1) // n_units
                    gen = act_stream(p_)

                    def filler():
                        for _k in range(per):
                            next(gen, None)
                    for ti in range(p_ * TPP, (p_ + 1) * TPP):
                        route_tile(ti, filler)
                    for _ in gen:
                        pass
                S.end()

        if cfg.stages >= 5:
            with ExitStack() as st:
                S.begin()
                lsb = lambda nm, shape, dt=F32: st.enter_context(nc.sbuf_tensor(un(nm), list(shape), dt))
                NB5 = 3
                NEC = NE // P
                wa = [lsb("w_a", [P, N], BF16) for _ in range(NB5)]
                wb_ = [lsb("w_b", [P, N], BF16) for _ in range(NB5)]
                wc = [lsb("w_c", [P, N], BF16) for _ in range(NB5)]
                t_wa, t_wc = S.toks("wwa", NB5), S.toks("wwc", NB5)

                def ldw5(c):
                    sl = c % NB5
                    S.dma('sp', wa[sl][:], s_wt[c * P:(c + 1) * P, 0:N], writes=[t_wa[sl]], semtok=t_wa[sl])
                    S.dma('sp', wb_[sl][:], s_gg[c * P:(c + 1) * P, 0:N], writes=[t_wa[sl]], semtok=t_wa[sl])
                for c in range(min(NB5 - 1, NEC)):
                    ldw5(c)
                for c in range(NEC):
                    if c + NB5 - 1 < NEC:
                        ldw5(c + NB5 - 1)
                    sl = c % NB5
                    S.op('dve' if c % 2 else 'pool',
                         lambda e: e.tensor_tensor(out=wc[sl][:], in0=wa[sl][:], in1=wb_[sl][:], op=ALU.mult),
                         reads=[t_wa[sl]], writes=[t_wc[sl]])
                    S.dma('act', s_wg[c * P:(c + 1) * P, 0:N], wc[sl][:], reads=[t_wc[sl]], semtok=t_wc[sl])
                S.end()

        if cfg.stages >= 5:
            with ExitStack() as st:
                S.begin()
                lsb = lambda nm, shape, dt=F32: st.enter_context(nc.sbuf_tensor(un(nm), list(shape), dt))
                cgw = min(512, D)
                NB3 = 3
                PT = min(8, NTT)
                vst = [lsb("v_st", [P, cgw]) for _ in range(NB3)]
                vb = [lsb("v_b", [P, cgw], BF16) for _ in range(NB3)]
                wgt = [lsb("v_wg", [P, PT * P], BF16) for _ in range(NB3)]
                xt = [lsb("v_xt", [P, cgw]) for _ in range(2)]
                ot = [lsb("v_ot", [P, cgw]) for _ in range(2)]
                t_vst, t_vb, t_wgt = S.toks("vst", NB3), S.toks("vb", NB3), S.toks("vwg", NB3)
                t_xt, t_ot = S.toks("vxt", 2), S.toks("vot", 2)
                NEC = NE // P
                k = 0
                for pp in range(NTT // PT):
                    for cg in range(D // cgw):
                        def ld(ec):
                            sl = ec % NB3
                            S.dma('sp', vst[sl][:], peer_v[ec * P:(ec + 1) * P, cg * cgw:(cg + 1) * cgw], writes=[t_vst[sl]],
                                  semtok=t_vst[sl])
                            S.dma('sp', wgt[sl][:], s_wg[ec * P:(ec + 1) * P, pp * PT * P:(pp + 1) * PT * P],
                                  writes=[t_wgt[sl]], semtok=t_wgt[sl])
                        for ec in range(NB3 - 1):
                            ld(ec)
                        for ec in range(NEC):
                            if ec + NB3 - 1 < NEC:
                                ld(ec + NB3 - 1)
                            sl = ec % NB3
                            S.op('dve' if ec % 2 else 'pool', lambda e: e.tensor_copy(out=vb[sl][:], in_=vst[sl][:]),
                                 reads=[t_vst[sl]], writes=[t_vb[sl]])

                            def mm(e):
                                for tt in range(PT):
                                    ins = e.matmul(PS[tt][:, :cgw], lhsT=wgt[sl][:, tt * P:(tt + 1) * P], rhs=vb[sl][:],
                                                   start=(ec == 0), stop=(ec == NEC - 1))
                                return ins
                            S.op('pe', mm, reads=[t_vb[sl], t_wgt[sl]], writes=tPS[0:PT])
                        for tt in range(PT):
                            sl = k % 2
                            k += 1
                            r0 = (pp * PT + tt) * P
                            S.dma('sp', xt[sl][:], s_x2[r0:r0 + P, cg * cgw:(cg + 1) * cgw], writes=[t_xt[sl]],
                                  semtok=t_xt[sl])
                            S.op('dve', lambda e: e.tensor_tensor(out=ot[sl][:], in0=PS[tt][:, :cgw], in1=xt[sl][:],
                                                                  op=ALU.add),
                                 reads=[tPS[tt], t_xt[sl]], writes=[t_ot[sl]])
                            S.dma('pool', s_x3[r0:r0 + P, cg * cgw:(cg + 1) * cgw], ot[sl][:], reads=[t_ot[sl]],
                                  semtok=t_ot[sl])
                S.end()

        if cfg.stages >= 6:
            with ExitStack() as st:
                S.begin()
                lsb = lambda nm, shape, dt=F32: st.enter_context(nc.sbuf_tensor(un(nm), list(shape), dt))
                gb = lsb("f_gb", [P, D])
                xs = [lsb("f_xs", [P, D]) for _ in range(2)]
                ob = [lsb("f_ob", [P, D]) for _ in range(2)]
                junk = lsb("f_junk", [P, D], BF16)
                ss = lsb("f_ss", [P, 4])
                t_gb, t_junk, t_ss = S.tok("fgb"), S.tok("fjunk"), S.tok("fss")
                t_xs, t_ob = S.toks("fxs", 2), S.toks("fob", 2)
                S.dma('sp', gb[:], final_norm[0:1, :].to_broadcast([P, D]), writes=[t_gb], semtok=t_gb)
                S.dma('sp', xs[0][:], s_x3[0:P, :], writes=[t_xs[0]], semtok=t_xs[0])
                for i in range(NTT):
                    sl = i % 2
                    if i + 1 < NTT:
                        S.dma('sp', xs[1 - sl][:], s_x3[(i + 1) * P:(i + 2) * P, :], writes=[t_xs[1 - sl]],
                              semtok=t_xs[1 - sl])
                    S.op('act', lambda e: e.activation(out=junk[:], in_=xs[sl][:], func=AF.Square, accum_out=ss[:, 0:1]),
                         reads=[t_xs[sl]], writes=[t_junk, t_ss])
                    S.op('act', lambda e: e.activation(out=ss[:, 1:2], in_=ss[:, 0:1], func=AF.Sqrt, scale=1.0 / D,
                                                       bias=epsT[:, 0:1]), reads=[t_ss, t_c], writes=[t_ss])
                    S.op('dve', lambda e: e.reciprocal(out=ss[:, 2:3], in_=ss[:, 1:2]), reads=[t_ss], writes=[t_ss])
                    S.op('dve', lambda e: e.scalar_tensor_tensor(out=ob[sl][:], in0=xs[sl][:], scalar=ss[:, 2:3], in1=gb[:],
                                                                 op0=ALU.mult, op1=ALU.mult),
                         reads=[t_xs[sl], t_ss, t_gb], writes=[t_ob[sl]])
                    S.dma('pool', out[i * P:(i + 1) * P, :], ob[sl][:], reads=[t_ob[sl]], semtok=t_ob[sl])
                S.end()

        S.finish()
    return nc


def make_consts():
    s = np.arange(P)[:, None]
    t = np.arange(P)[None, :]
    ident = (s == t).astype(np.float32)
    triU = (s <= t).astype(np.float32) * (-1.0 / 16.0)
    triL = (s > t).astype(np.float32) * (-1.0 / 16.0)
    mask = (s <= t).astype(np.float32)
    ones = np.ones((P, P), np.float32)
    return np.ascontiguousarray(np.concatenate([ident, triU, triL, mask, ones], axis=1))


def run(cfg, inp, dbg=()):
    N, NPRE, D = cfg.N, cfg.NPRE, cfg.D
    x = np.asarray(inp["x"], np.float32)
    B, S, _ = x.shape
    halves = S // N
    ncores = B * halves
    assert ncores == 8
    f = lambda a: np.ascontiguousarray(np.asarray(a, np.float32))
    shared = {
        "norm_mix": f(inp["norm_mix"]).reshape(1, D),
        "w_in": f(inp["w_in"][0]),
        "w_aug": f(np.concatenate([np.asarray(inp["w_a_up"][0]), np.asarray(inp["b_a"][0])[None, :]], axis=0)),
        "gla_norm": f(inp["gla_norm"][0]).reshape(4, P),
        "conv_w": f(inp["conv_w"][0]).reshape(3 * cfg.CW // P, P),
        "w_br_gla": f(inp["w_br_gla"][0]),
        "w_br_conv": f(inp["w_br_conv"][0]),
        "w_mem_kv": f(inp["w_mem_kv"][0]),
        "w_br_xa": f(inp["w_br_xa"][0]),
        "b_gate": f(inp["b_gate"][0]).reshape(3 * D // P, P),
        "w_o": f(inp["w_o"][0]),
        "mem_norm": f(inp["mem_norm"]).reshape(1, D),
        "norm_ffn": f(inp["norm_ffn"]).reshape(1, D),
        "peer_wq": f(inp["peer_wq"][0]),
        "peer_sk": f(inp["peer_subkeys"][0]).reshape(2 * cfg.PH * P, P),
        "peer_u": f(inp["peer_u"][0]),
        "peer_v": f(inp["peer_v"][0]),
        "final_norm": f(inp["final_norm"]).reshape(1, D),
        "cst": make_consts(),
    }
    zeros_pre = np.zeros((NPRE, D), np.float32)
    in_maps = []
    for c in range(ncores):
        b, h = c // halves, c % halves
        m = dict(shared)
        m["x_main"] = f(x[b, h * N:(h + 1) * N])
        m["x_pre"] = f(x[b, h * N - NPRE:h * N]) if h > 0 else zeros_pre
        m["mem"] = f(inp["mem"][b])
        in_maps.append(m)
    nc = build(cfg, dbg=dbg)
    res = run_bass_kernel_spmd(nc, in_maps, core_ids=list(range(ncores)))
    return res.results


def kernel(**inputs):
    cfg = Cfg()
    r = run(cfg, inputs)
    x = inputs["x"]
    B, S, D = x.shape
    out = np.empty((B, S, D), np.float32)
    halves = S // cfg.N
    for c in range(B * halves):
        b, h = c // halves, c % halves
        out[b, h * cfg.N:(h + 1) * cfg.N] = r[c]["out"]
    return out
```
